# Optimizing a Trainium2 kernel written in Bass

```python
import math
import jax
import jax.numpy as jnp
from jax import lax
import numpy as np

D_MODEL = 2048
BATCH = 4
SEQ = 2048
DEPTH = 4
DEC_BATCH = 8
DEC_SEQ = 1
PAST_LEN = 16384
PAGE_SIZE = 128

HEAD_DIM = 64
N_HEADS = D_MODEL // HEAD_DIM
N_ATT_HEADS = N_HEADS // 2
N_RWKV_HEADS = N_HEADS - N_ATT_HEADS
D_ATT = N_ATT_HEADS * HEAD_DIM
D_RWKV = N_RWKV_HEADS * HEAD_DIM
QKV_COLS = 3 * D_ATT + 3 * D_RWKV
BRANCHES = ((128, 1), (512, 4), (2048, 16))
W_MAX = max(w for w, _ in BRANCHES)
BAND = 128
NUM_BUCKETS = 32
MAX_DISTANCE = W_MAX
ATT_SCALE = HEAD_DIM ** -0.5
D_FF = -(-8 * D_MODEL // (3 * 256)) * 256
D_DECAY_LORA = 64
D_AAA_LORA = 64
D_MV_LORA = 32
D_GATE_LORA = 160
RMS_EPS = 1e-6
GN_EPS = 64e-5

kernel_name = 'hymba_dilated_attn_rwkv7_adaln_step'


def rmsnorm(x, g):
    xf = x.astype(jnp.float32)
    y = xf * lax.rsqrt(jnp.mean(xf * xf, axis=-1, keepdims=True) + RMS_EPS)
    return (y * g.astype(jnp.float32)).astype(x.dtype)


def modulate(h, shift, scale):
    return h * (1 + scale[:, None, :]) + shift[:, None, :]


def token_shift(t, prev):
    return jnp.concatenate([prev[:, None, :].astype(t.dtype), t[:, :-1]], axis=1)


def rel_bucket(dist):
    max_exact = NUM_BUCKETS // 2
    d = jnp.maximum(dist, 0)
    df = jnp.maximum(d, 1).astype(jnp.float32)
    large = max_exact + (jnp.log(df / max_exact) / math.log(MAX_DISTANCE / max_exact)
                         * (NUM_BUCKETS - max_exact)).astype(jnp.int32)
    return jnp.where(d < max_exact, d, jnp.minimum(large, NUM_BUCKETS - 1))


def dilated_branch_prompt(q, k, v, rel_bias, window, dil):
    B, S, H, E = q.shape
    L = S // dil
    nb = -(-L // BAND)
    Lp = nb * BAND
    wsub = window // dil

    def to_res(t):
        t = t.reshape(B, L, dil, H, E).transpose(0, 2, 1, 3, 4)
        t = jnp.pad(t, ((0, 0), (0, 0), (0, Lp - L), (0, 0), (0, 0)))
        return t.reshape(B, dil, nb, BAND, H, E)

    def with_prev(t):
        prev = jnp.pad(t, ((0, 0), (0, 0), (1, 0), (0, 0), (0, 0), (0, 0)))[:, :, :-1]
        return jnp.concatenate([prev, t], axis=3)

    qb = to_res(q)
    kb = with_prev(to_res(k))
    vb = with_prev(to_res(v))
    logits = jnp.einsum('brnqhe,brnkhe->brnhqk', qb, kb) * ATT_SCALE
    iq = jnp.arange(BAND)[:, None]
    jk = jnp.arange(2 * BAND)[None, :]
    dsub = BAND + iq - jk
    kpos = (jnp.arange(nb)[:, None, None] - 1) * BAND + jk[None]
    valid = (dsub >= 0) & (dsub <= wsub) & (kpos >= 0)
    bias = rel_bias[rel_bucket(dsub * dil)].astype(jnp.float32)
    logits = logits + jnp.moveaxis(bias, -1, 0)
    logits = jnp.where(valid[:, None], logits, -jnp.inf)
    lse = jax.nn.logsumexp(logits, axis=-1)
    p = jnp.exp(logits - lse[..., None])
    o = jnp.einsum('brnhqk,brnkhe->brnqhe', p, vb)

    def from_res(t):
        t = t.reshape((B, dil, Lp) + t.shape[4:])[:, :, :L]
        t = jnp.swapaxes(t, 1, 2)
        return t.reshape((B, S) + t.shape[3:])

    return from_res(o), from_res(jnp.swapaxes(lse, 3, 4))


def dilated_branch_sample(q, kc, vc, rel_bias, window, dil, wbuf):
    T = q.shape[1]
    m = jnp.arange(window // dil + 1)
    idx = wbuf + jnp.arange(T)[:, None] - dil * m[None, :]
    valid = idx >= 0
    idxc = jnp.maximum(idx, 0)
    kg = kc[:, idxc]
    vg = vc[:, idxc]
    logits = jnp.einsum('bthe,btmhe->bhtm', q, kg) * ATT_SCALE
    bias = rel_bias[rel_bucket(dil * m)].astype(jnp.float32)
    logits = logits + bias.T[None, :, None, :]
    logits = jnp.where(valid[None, None], logits, -jnp.inf)
    lse = jax.nn.logsumexp(logits, axis=-1)
    p = jnp.exp(logits - lse[..., None])
    o = jnp.einsum('bhtm,btmhe->bthe', p, vg)
    return o, jnp.swapaxes(lse, 1, 2)


def wkv_scan(S0, r, w, k, v, a, b):
    def step(S, inp):
        r_t, w_t, k_t, v_t, a_t, b_t = inp
        sa = jnp.einsum('bhij,bhj->bhi', S, a_t)
        S = S * w_t[:, :, None, :] + sa[..., None] * b_t[:, :, None, :] + v_t[..., None] * k_t[:, :, None, :]
        return S, jnp.einsum('bhij,bhj->bhi', S, r_t)
    xs = tuple(jnp.swapaxes(t, 0, 1) for t in (r, w, k, v, a, b))
    S, ys = lax.scan(step, S0, xs)
    return S, jnp.swapaxes(ys, 0, 1)


def trunk_layer(P, l, x, c, h_last, wkv0, k_buf, v_buf, v_first):
    B, T, _ = x.shape
    f32 = jnp.float32
    mod = jnp.dot(jax.nn.silu(c), P['ada_w'][l]) + P['ada_b'][l]
    sh1, sc1, gt1, sh2, sc2, gt2 = jnp.split(mod, 6, axis=-1)
    h = modulate(rmsnorm(x, P['norm1_g'][l]), sh1, sc1)
    w_in = P['w_in'][l]
    qa, ka, va, rkv0 = jnp.split(h @ w_in, [D_ATT, 2 * D_ATT, 3 * D_ATT], axis=-1)

    q = rmsnorm(qa.reshape(B, T, N_ATT_HEADS, HEAD_DIM), P['q_norm_g'][l]).astype(f32)
    k = rmsnorm(ka.reshape(B, T, N_ATT_HEADS, HEAD_DIM), P['k_norm_g'][l])
    v = va.reshape(B, T, N_ATT_HEADS, HEAD_DIM)
    if k_buf is None:
        res = [dilated_branch_prompt(q, k.astype(f32), v.astype(f32), P['rel_bias'], w, d)
               for (w, d) in BRANCHES]
        keep = min(W_MAX, T)
        new_k, new_v = k[:, T - keep:], v[:, T - keep:]
    else:
        kc = jnp.concatenate([k_buf.astype(f32), k.astype(f32)], axis=1)
        vc = jnp.concatenate([v_buf.astype(f32), v.astype(f32)], axis=1)
        res = [dilated_branch_sample(q, kc, vc, P['rel_bias'], w, d, k_buf.shape[1])
               for (w, d) in BRANCHES]
        new_k, new_v = k, v
    outs = jnp.stack([o for o, _ in res])
    lses = jnp.stack([s for _, s in res])
    alpha = jax.nn.softmax(lses, axis=0)
    att = jnp.einsum('nbth,nbthe->bthe', alpha, outs).reshape(B, T, D_ATT).astype(x.dtype)

    H, N = N_RWKV_HEADS, HEAD_DIM
    dh = token_shift(h, h_last) - h
    mu = P['mu_wag'][l]
    xw = h + dh * mu[0]
    xa = h + dh * mu[1]
    xg = h + dh * mu[2]
    rkv_prev = token_shift(rkv0, h_last @ w_in[:, 3 * D_ATT:])
    rkv = rkv0 + (rkv_prev - rkv0) * P['mu_rkv'][l].reshape(-1)
    r, kr, vr = jnp.split(rkv.astype(f32), 3, axis=-1)
    logw = -jax.nn.softplus(-(P['decay_w0'][l] + jnp.tanh(xw @ P['decay_w1'][l]) @ P['decay_w2'][l])) - 0.5
    decay = jnp.exp(-jnp.exp(logw.astype(f32)))
    if l == 0:
        v_first = vr
    else:
        xv = h + dh * P['vres_mu'][l - 1]
        vgate = jax.nn.sigmoid((P['vres_v0'][l - 1] + (xv @ P['vres_w1'][l - 1]) @ P['vres_w2'][l - 1]).astype(f32))
        vr = vr + (v_first - vr) * vgate
    a = jax.nn.sigmoid((P['aaa_a0'][l] + (xa @ P['aaa_w1'][l]) @ P['aaa_w2'][l]).astype(f32))
    g = (jax.nn.sigmoid(xg @ P['gate_w1'][l]) @ P['gate_w2'][l]).astype(f32)
    kk = (kr * P['k_k'][l]).reshape(B, T, H, N)
    kk = kk / jnp.maximum(jnp.sqrt(jnp.sum(kk * kk, axis=-1, keepdims=True)), 1e-12)
    kr = kr * (1 + (a - 1) * P['k_a'][l])
    rh = r.reshape(B, T, H, N)
    kh = kr.reshape(B, T, H, N)
    vh = vr.reshape(B, T, H, N)
    ah = a.reshape(B, T, H, N)
    wkv, y = wkv_scan(wkv0.astype(f32), rh, decay.reshape(B, T, H, N), kh, vh, -kk, kk * ah)
    y_mu = jnp.mean(y, axis=-1, keepdims=True)
    y_var = jnp.mean(jnp.square(y - y_mu), axis=-1, keepdims=True)
    y = (y - y_mu) * lax.rsqrt(y_var + GN_EPS)
    y = y * P['lnx_g'][l].reshape(H, N) + P['lnx_b'][l].reshape(H, N)
    y = y + jnp.sum(rh * kh * P['r_k'][l], axis=-1, keepdims=True) * vh
    rw = (y.reshape(B, T, D_RWKV) * g).astype(x.dtype)

    mix = jnp.concatenate([att, rw], axis=-1) @ P['w_out'][l]
    x = x + gt1[:, None, :] * mix

    h2 = modulate(rmsnorm(x, P['norm2_g'][l]), sh2, sc2)
    gate, up = jnp.split(h2 @ P['w_gu'][l], 2, axis=-1)
    x = x + gt2[:, None, :] * ((jax.nn.silu(gate) * up) @ P['w_down'][l])
    return x, v_first, new_k, new_v, wkv.astype(x.dtype), h[:, -1]


def setup_inputs(seed: int = 0) -> dict:
    key = jax.random.key(seed)
    ks = iter(jax.random.split(key, 48))

    def nrm(shape, scale):
        return jax.random.normal(next(ks), shape, jnp.float32) * scale

    def uni(shape, lo, hi):
        return jax.random.uniform(next(ks), shape, jnp.float32, lo, hi)

    wbuf = min(W_MAX, PAST_LEN)
    nv = DEPTH - 1
    d_in = D_MODEL ** -0.5
    return {
        'x_prompt': nrm((BATCH, SEQ, D_MODEL), 1.0),
        'x_sample': nrm((DEC_BATCH, DEC_SEQ, D_MODEL), 1.0),
        'c_prompt': nrm((BATCH, D_MODEL), 1.0),
        'c_sample': nrm((DEC_BATCH, D_MODEL), 1.0),
        'cache_k': nrm((DEPTH, DEC_BATCH, wbuf, N_ATT_HEADS, HEAD_DIM), 1.0),
        'cache_v': nrm((DEPTH, DEC_BATCH, wbuf, N_ATT_HEADS, HEAD_DIM), 1.0),
        'state_wkv': nrm((DEPTH, DEC_BATCH, N_RWKV_HEADS, HEAD_DIM, HEAD_DIM), 0.3),
        'state_shift': nrm((DEPTH, DEC_BATCH, D_MODEL), 1.0),
        'rel_bias': nrm((NUM_BUCKETS, N_ATT_HEADS), 0.5),
        'ada_w': nrm((DEPTH, D_MODEL, 6 * D_MODEL), 0.5 * d_in),
        'ada_b': nrm((DEPTH, 6 * D_MODEL), 0.02),
        'norm1_g': 1.0 + nrm((DEPTH, D_MODEL), 0.02),
        'norm2_g': 1.0 + nrm((DEPTH, D_MODEL), 0.02),
        'w_in': nrm((DEPTH, D_MODEL, QKV_COLS), d_in),
        'q_norm_g': 1.0 + nrm((DEPTH, HEAD_DIM), 0.02),
        'k_norm_g': 1.0 + nrm((DEPTH, HEAD_DIM), 0.02),
        'mu_wag': uni((DEPTH, 3, D_MODEL), 0.0, 1.0),
        'mu_rkv': uni((DEPTH, 3, D_RWKV), 0.0, 1.0),
        'decay_w0': uni((DEPTH, D_RWKV), -3.0, 0.0),
        'decay_w1': nrm((DEPTH, D_MODEL, D_DECAY_LORA), d_in),
        'decay_w2': nrm((DEPTH, D_DECAY_LORA, D_RWKV), 0.1 * D_DECAY_LORA ** -0.5),
        'aaa_a0': nrm((DEPTH, D_RWKV), 0.1),
        'aaa_w1': nrm((DEPTH, D_MODEL, D_AAA_LORA), d_in),
        'aaa_w2': nrm((DEPTH, D_AAA_LORA, D_RWKV), 0.1 * D_AAA_LORA ** -0.5),
        'gate_w1': nrm((DEPTH, D_MODEL, D_GATE_LORA), d_in),
        'gate_w2': nrm((DEPTH, D_GATE_LORA, D_RWKV), D_GATE_LORA ** -0.5),
        'vres_mu': uni((nv, D_MODEL), 0.0, 1.0),
        'vres_v0': 1.0 + nrm((nv, D_RWKV), 0.1),
        'vres_w1': nrm((nv, D_MODEL, D_MV_LORA), d_in),
        'vres_w2': nrm((nv, D_MV_LORA, D_RWKV), 0.1 * D_MV_LORA ** -0.5),
        'k_k': 0.85 + nrm((DEPTH, D_RWKV), 0.02),
        'k_a': 1.0 + nrm((DEPTH, D_RWKV), 0.02),
        'r_k': nrm((DEPTH, N_RWKV_HEADS, HEAD_DIM), 0.1),
        'lnx_g': 1.0 + nrm((DEPTH, D_RWKV), 0.02),
        'lnx_b': nrm((DEPTH, D_RWKV), 0.02),
        'w_out': nrm((DEPTH, D_MODEL, D_MODEL), d_in),
        'w_gu': nrm((DEPTH, D_MODEL, 2 * D_FF), d_in),
        'w_down': nrm((DEPTH, D_FF, D_MODEL), D_FF ** -0.5),
    }


def reference(x_prompt, x_sample, c_prompt, c_sample, cache_k, cache_v, state_wkv, state_shift,
              rel_bias, ada_w, ada_b, norm1_g, norm2_g, w_in, q_norm_g, k_norm_g, mu_wag, mu_rkv,
              decay_w0, decay_w1, decay_w2, aaa_a0, aaa_w1, aaa_w2, gate_w1, gate_w2,
              vres_mu, vres_v0, vres_w1, vres_w2, k_k, k_a, r_k, lnx_g, lnx_b, w_out, w_gu, w_down):
    P = dict(rel_bias=rel_bias, ada_w=ada_w, ada_b=ada_b, norm1_g=norm1_g, norm2_g=norm2_g,
             w_in=w_in, q_norm_g=q_norm_g, k_norm_g=k_norm_g, mu_wag=mu_wag, mu_rkv=mu_rkv,
             decay_w0=decay_w0, decay_w1=decay_w1, decay_w2=decay_w2,
             aaa_a0=aaa_a0, aaa_w1=aaa_w1, aaa_w2=aaa_w2, gate_w1=gate_w1, gate_w2=gate_w2,
             vres_mu=vres_mu, vres_v0=vres_v0, vres_w1=vres_w1, vres_w2=vres_w2,
             k_k=k_k, k_a=k_a, r_k=r_k, lnx_g=lnx_g, lnx_b=lnx_b,
             w_out=w_out, w_gu=w_gu, w_down=w_down)

    B = x_prompt.shape[0]
    h0 = jnp.zeros((B, D_MODEL), x_prompt.dtype)
    s0 = jnp.zeros((B, N_RWKV_HEADS, HEAD_DIM, HEAD_DIM), jnp.float32)
    xp, vf = x_prompt, None
    pk, pv, ps, ph = [], [], [], []
    for l in range(DEPTH):
        xp, vf, nk, nvv, ns, nh = trunk_layer(P, l, xp, c_prompt, h0, s0, None, None, vf)
        pk.append(nk)
        pv.append(nvv)
        ps.append(ns)
        ph.append(nh)

    xs, vf = x_sample, None
    sk, sv, ss, sh = [], [], [], []
    for l in range(DEPTH):
        xs, vf, nk, nvv, ns, nh = trunk_layer(P, l, xs, c_sample, state_shift[l], state_wkv[l],
                                              cache_k[l], cache_v[l], vf)
        sk.append(nk)
        sv.append(nvv)
        ss.append(ns)
        sh.append(nh)

    return (xp, xs, jnp.stack(pk), jnp.stack(pv), jnp.stack(ps), jnp.stack(ph),
            jnp.stack(sk), jnp.stack(sv), jnp.stack(ss), jnp.stack(sh))
```

```python
import math
import numpy as np
from contextlib import ExitStack
import concourse.bass as bass
import concourse.mybir as mybir
from concourse.bass_utils import run_bass_kernel_spmd

F32 = mybir.dt.float32
F32R = mybir.dt.float32r
BF16 = mybir.dt.bfloat16
AF = mybir.ActivationFunctionType
ALU = mybir.AluOpType
AX = mybir.AxisListType

D = 2048
T = 2048
NL = 4
TH = 1024
RG = [[0, 4], [1, 5], [2, 6], [3, 7]]
NCOL = 2051
SCOL = 2050
NTILES = [(0, 512), (512, 512), (1024, 512), (1536, 512), (2048, 3)]
DFF = 5632
EPS = 1e-6
GN_EPS = 64e-5
EW = 2176

VOFF = {}
_o = 0
for _n, _w in [("ada_b", 96), ("g1", 16), ("g2", 16), ("mu_w", 16), ("mu_a", 16), ("mu_g", 16),
               ("mu_r", 8), ("mu_k", 8), ("mu_v", 8), ("w0", 8), ("a0", 8), ("vmu", 16), ("v0", 8),
               ("k_k", 8), ("k_a", 8), ("r_k", 8), ("lnx_g", 8), ("lnx_b", 8), ("gq", 1), ("gk", 1)]:
    VOFF[_n] = (_o, _w)
    _o += _w
NV = _o


class Res:
    __slots__ = ("name", "w", "r", "excl")

    def __init__(self, name, excl=False):
        self.name = name
        self.w = {}
        self.r = {}
        self.excl = excl


class Sched:
    ENG = ("pe", "act", "dve", "pool", "sp")
    NDS = 8

    def __init__(self, nc, es):
        self.nc = nc
        self.q = {e: [] for e in self.ENG}
        self.sem = {e: es.enter_context(nc.semaphore("S_" + e)) for e in self.ENG}
        self.cnt = {e: 0 for e in self.ENG}
        self.seen = {e: {} for e in self.ENG}
        self.pend = {e: ([], []) for e in self.ENG}
        self.dsem = {e: [es.enter_context(nc.semaphore("D_%s%d" % (e, i))) for i in range(self.NDS)]
                     for e in ("sp", "pool")}
        self.dval = {e: [0] * self.NDS for e in ("sp", "pool")}
        self.dnext = {e: 0 for e in ("sp", "pool")}
        self.semobj = {}
        for e in self.ENG:
            self.semobj["S_" + e] = self.sem[e]
        for e in ("sp", "pool"):
            for i in range(self.NDS):
                self.semobj["D_%s%d" % (e, i)] = self.dsem[e][i]
        self.ninstr = 0

    def _deps(self, eng, reads, writes):
        need = {}
        for r in reads:
            for k, v in r.w.items():
                if need.get(k, 0) < v:
                    need[k] = v
        for w in writes:
            for k, v in w.w.items():
                if need.get(k, 0) < v:
                    need[k] = v
            for k, v in w.r.items():
                if need.get(k, 0) < v:
                    need[k] = v
        waits = []
        seen = self.seen[eng]
        own = "S_" + eng
        for k, v in need.items():
            if k == own and eng == "pe":
                continue
            if seen.get(k, 0) >= v:
                continue
            seen[k] = v
            waits.append((self.semobj[k], v))
        return waits

    def op(self, eng, fn, reads=(), writes=(), sig=True):
        ex = [r for r in reads if r.excl]
        if ex:
            writes = list(writes) + ex
        waits = self._deps(eng, reads, writes)
        pr, pw = self.pend[eng]
        pr.extend(reads)
        pw.extend(writes)
        inc = None
        if sig:
            self.cnt[eng] += 1
            key = "S_" + eng
            val = self.cnt[eng]
            for r in pr:
                r.r[key] = val
            for w in pw:
                w.w = {key: val}
                w.r = {}
            self.pend[eng] = ([], [])
            inc = (self.sem[eng], 1)
        self.q[eng].append((waits, fn, inc))
        self.ninstr += 1

    def dma(self, qn, out, in_, reads=(), writes=(), **kw):
        k = self.dnext[qn]
        self.dnext[qn] = (k + 1) % (2 if qn == "pool" else self.NDS)
        key = "D_%s%d" % (qn, k)
        waits = self._deps(qn, reads, writes)
        prev = self.dval[qn][k]
        if prev > 0 and self.seen[qn].get(key, 0) < prev:
            self.seen[qn][key] = prev
            waits.append((self.semobj[key], prev))
        val = prev + 16
        self.dval[qn][k] = val
        for r in reads:
            r.r[key] = val
        for w in writes:
            w.w = {key: val}
            w.r = {}
        self.q[qn].append((waits, (lambda e: e.dma_start(out=out, in_=in_, **kw)), (self.semobj[key], 16)))
        self.ninstr += 1

    def barrier(self):
        cur = {}
        for e in self.ENG:
            if self.cnt[e] > 0:
                cur["S_" + e] = self.cnt[e]
        for e in ("sp", "pool"):
            for i in range(self.NDS):
                if self.dval[e][i] > 0:
                    cur["D_%s%d" % (e, i)] = self.dval[e][i]
        for e in self.ENG:
            waits = []
            for k, v in cur.items():
                if k == "S_" + e and e == "pe":
                    continue
                if self.seen[e].get(k, 0) < v:
                    self.seen[e][k] = v
                    waits.append((self.semobj[k], v))
            if waits:
                self.q[e].append((waits, None, None))

    def emit(self, e, name):
        for waits, fn, inc in self.q[name]:
            for sem, val in waits:
                e.wait_ge(sem, val)
            if fn is None:
                continue
            ins = fn(e)
            if inc is not None:
                ins.then_inc(inc[0], inc[1])


def bucket_np(d):
    d = np.asarray(d)
    dd = np.maximum(d, 0)
    df = np.maximum(dd, 1).astype(np.float32)
    large = 16 + (np.log(df / np.float32(16)) / np.float32(math.log(2048 / 16)) * np.float32(16)).astype(np.int32)
    return np.where(dd < 16, dd, np.minimum(large, 31))


def count_np(d):
    d = np.asarray(d)
    c = (d <= 128).astype(np.float32) + ((d % 4 == 0) & (d <= 512)) + ((d % 16 == 0) & (d <= 2048))
    return np.where(d >= 0, c, 0).astype(np.float32)


def host_consts():
    c = {}
    c["ident"] = np.eye(128, dtype=np.float32)
    bo = np.zeros((128, 128), np.float32)
    bo[:64, :64] = 1
    bo[64:, 64:] = 1
    c["blockones"] = bo
    s = np.arange(128)[:, None]
    t = np.arange(128)[None, :]
    c["m_lt"] = (s < t).astype(np.float32)
    c["m_le"] = (s <= t).astype(np.float32)
    c["m_gt"] = (s > t).astype(np.float32)
    dd = np.arange(EW) - 127
    bk = bucket_np(np.maximum(dd, 0))
    oh = np.zeros((32, EW), np.float32)
    oh[bk, np.arange(EW)] = 1
    c["oh"] = oh
    c["cnt16"] = np.broadcast_to(count_np(dd)[None, :], (16, EW)).astype(np.float32).copy()
    ds = np.concatenate([2048 - np.arange(2048), [0]])
    ohs = np.zeros((32, 2049), np.float32)
    ohs[bucket_np(ds), np.arange(2049)] = 1
    c["ohs"] = ohs
    cs = count_np(ds)
    cnts = np.zeros((128, 17), np.float32)
    cnts[:, :16] = cs[:2048].reshape(16, 128).T
    cnts[0, 16] = cs[2048]
    c["cnts"] = cnts
    bd = np.zeros((16, 1024), np.float32)
    for h in range(16):
        bd[h, h * 64:(h + 1) * 64] = 1
    c["bdmask"] = bd
    return c


def fm(v, nch):
    return np.ascontiguousarray(np.asarray(v, np.float32).reshape(nch, 128).T)


def build(nl=NL, stop_after=None, dbg=False, skip=(), rw_c=8, rw_nch=16, rw_sample=True):
    nc = bass.Bass("TRN2", target_bir_lowering=False)
    es = ExitStack()
    K = Sched(nc, es)

    def din(name, shape, dt=F32):
        return nc.dram_tensor(name, list(shape), dt, kind="ExternalInput").ap()

    def dout(name, shape, dt=F32):
        return nc.dram_tensor(name, list(shape), dt, kind="ExternalOutput").ap()

    def dscr(name, shape, dt=F32):
        return nc.dram_tensor(name, list(shape), dt).ap()

    uniq = [0]

    def sb(name, shape, dt=F32, stack=es):
        uniq[0] += 1
        return stack.enter_context(nc.sbuf_tensor("s%d_%s" % (uniq[0], name), list(shape), dt))

    xp = din("xp", [T, D])
    xph = din("xph", [TH, D])
    sel_d = din("sel", [128, 2])
    w_out_perm = din("w_out_perm", [NL, D, D])
    xs = din("xs", [1, D])
    cT_d = din("cT", [128, 16, 2])
    ck_d = din("ck", [NL, 2048, 1024])
    cv_d = din("cv", [NL, 2048, 1024])
    swkv_d = din("swkv", [NL, 16, 64, 64])
    sshT_d = din("sshT", [NL, 128, 16])
    relb_d = din("relb", [32, 16])
    vecs_d = din("vecs", [128, NL, NV])
    ada_w = din("ada_w", [NL, D, 6 * D])
    w_in = din("w_in", [NL, D, 6144])
    w_out = din("w_out", [NL, D, D])
    w_gu = din("w_gu", [NL, D, 2 * DFF])
    w_down = din("w_down", [NL, DFF, D])
    dw1 = din("decay_w1", [NL, D, 64])
    dw2 = din("decay_w2", [NL, 64, 1024])
    aw1 = din("aaa_w1", [NL, D, 64])
    aw2 = din("aaa_w2", [NL, 64, 1024])
    gw1 = din("gate_w1", [NL, D, 160])
    gw2 = din("gate_w2", [NL, 160, 1024])
    vw1 = din("vres_w1", [3, D, 32])
    vw2 = din("vres_w2", [3, 32, 1024])
    cst = {}
    for n, shp in [("ident", [128, 128]), ("blockones", [128, 128]), ("m_lt", [128, 128]), ("m_le", [128, 128]),
                   ("m_gt", [128, 128]), ("oh", [32, EW]), ("cnt16", [16, EW]), ("ohs", [32, 2049]),
                   ("cnts", [128, 17]), ("bdmask", [16, 1024])]:
        cst[n] = din("c_" + n, shp)

    y_p = dout("y_p", [TH, D])
    y_s = dout("y_s", [1, D])
    nk_p = dout("nk_p", [NL, T, 512])
    nv_p = dout("nv_p", [NL, T, 512])
    nwkv_p = dout("nwkv_p", [NL, 8, 64, 64])
    nsh_p = dout("nsh_p", [NL, 128, 16])
    nk_s = dout("nk_s", [NL, 1, 1024])
    nv_s = dout("nv_s", [NL, 1, 1024])
    nwkv_s = dout("nwkv_s", [NL, 16, 64, 64])
    nsh_s = dout("nsh_s", [NL, 128, 16])

    xg = [dscr("xg%d" % k, [512, D]) for k in range(4)]
    xhp = [dscr("xhp%d" % k, [256, D]) for k in range(4)]
    R_cc = Res("cc")
    xsb = dscr("xsb", [1, D])
    x1buf = dscr("x1buf", [TH + 1, D])
    estrip = dscr("estrip", [16, 128, 2048])
    mixh = [dscr("mixh%d" % i, [8 * 128, TH], BF16) for i in range(2)]
    mixg = [dscr("mixg%d" % i, [16 * 128, TH], BF16) for i in range(2)]
    R_mixh, R_mixg, R_xfull, R_xh = Res("mixh"), Res("mixg"), Res("xfull"), Res("xh")
    vfirst_d = dscr("vfirst_d", [8, 128, NCOL])
    R_xbuf, R_x1buf, R_estrip, R_vfirst = Res("xbuf"), Res("x1buf"), Res("estrip"), Res("vfirst")
    R_out = Res("outs")

    ident = sb("ident", [128, 128])
    blockones = sb("blockones", [128, 128])
    m_lt = sb("m_lt", [128, 128])
    m_le = sb("m_le", [128, 128])
    m_gt = sb("m_gt", [128, 128])
    vecs = sb("vecs", [128, NL, NV])
    EsS = sb("EsS", [128, 17, 16])
    ones_f = sb("ones_f", [128, 128])
    sc_bf = sb("sc_bf", [128, 16, 2], BF16)
    hT = sb("hT", [128, 16, NCOL], BF16)
    R_c = Res("consts")
    R_hT = Res("hT")
    psb = [es.enter_context(nc.psum_tensor("ps%d" % i, [128, 512], F32)) for i in range(8)]
    R_ps = [Res("ps%d" % i, excl=True) for i in range(8)]

    for t_, n in [(ident, "ident"), (blockones, "blockones"), (m_lt, "m_lt"), (m_le, "m_le"), (m_gt, "m_gt")]:
        K.dma("sp", t_[:], cst[n], writes=[R_c])
    K.dma("sp", vecs[:], vecs_d, writes=[R_c])
    K.op("dve", lambda e: e.memset(ones_f[:], 1.0), writes=[R_c])
    K.op("dve", lambda e: e.memset(hT[:, :, 0:1], 0.0), writes=[R_hT])

    def V(l, name, j=None):
        o, w = VOFF[name]
        if j is None:
            return vecs[:, l, o:o + w]
        return vecs[:, l, o + j:o + j + 1]

    with ExitStack() as p0:
        oh = sb("oh", [32, EW], stack=p0)
        cnt16 = sb("cnt16", [16, EW], stack=p0)
        ohs = sb("ohs", [32, 2049], stack=p0)
        cnts = sb("cnts", [128, 17], stack=p0)
        relb = sb("relb", [32, 16], stack=p0)
        Esb = sb("Esb", [16, EW], stack=p0)
        cT = sb("cTs", [128, 16, 2], stack=p0)
        R0 = Res("p0")
        R_E = Res("Esb")
        for t_, src in [(oh, cst["oh"]), (cnt16, cst["cnt16"]), (ohs, cst["ohs"]), (cnts, cst["cnts"]),
                        (relb, relb_d), (cT, cT_d)]:
            K.dma("sp", t_[:], src, writes=[R0])
        K.op("act", lambda e: e.activation(out=sc_bf[:], in_=cT[:], func=AF.Silu), reads=[R0], writes=[R_c])
        c0 = 0
        bi = 0
        while c0 < EW:
            n = min(512, EW - c0)
            ps = psb[bi % 2]
            K.op("pe", lambda e, ps=ps, c0=c0, n=n: e.matmul(ps[0:16, 0:n], lhsT=relb[:, :], rhs=oh[:, c0:c0 + n],
                                                            start=True, stop=True),
                 reads=[R0], writes=[R_ps[bi % 2]])
            K.op("act", lambda e, ps=ps, c0=c0, n=n: e.activation(out=Esb[:, c0:c0 + n], in_=ps[0:16, 0:n], func=AF.Exp),
                 reads=[R_ps[bi % 2]], writes=[R_E])
            c0 += n
            bi += 1
        K.op("dve", lambda e: e.tensor_tensor(out=Esb[:], in0=Esb[:], in1=cnt16[:], op=ALU.mult),
             reads=[R0, R_E], writes=[R_E])
        for p in range(128):
            K.dma("sp", estrip[:, p, :], Esb[:, 127 - p:127 - p + 2048], reads=[R_E], writes=[Res('e')])
        for blk in range(17):
            m = 128 if blk < 16 else 1
            ps = psb[2 + blk % 2]
            K.op("pe", lambda e, ps=ps, blk=blk, m=m: e.matmul(ps[0:m, 0:16], lhsT=ohs[:, blk * 128:blk * 128 + m],
                                                              rhs=relb[:, :], start=True, stop=True),
                 reads=[R0], writes=[R_ps[2 + blk % 2]])
            K.op("act", lambda e, ps=ps, blk=blk, m=m: e.activation(out=EsS[0:m, blk, :], in_=ps[0:m, 0:16], func=AF.Exp,
                                                                   scale=1.0),
                 reads=[R_ps[2 + blk % 2]], writes=[R_c])
            K.op("dve", lambda e, blk=blk, m=m: e.tensor_scalar(out=EsS[0:m, blk, :], in0=EsS[0:m, blk, :],
                                                                scalar1=cnts[0:m, blk:blk + 1], scalar2=None,
                                                                op0=ALU.mult),
                 reads=[R0, R_c], writes=[R_c])
        K.barrier()


    SQD = math.sqrt(D)
    modt = sb("modt", [128, 2, 96])
    A1 = sb("A1", [128, 2, 16])
    A2 = sb("A2", [128, 2, 16])
    hl = sb("hl", [128, 2, 16])
    R_mod = Res("mod")
    R_hl = Res("hl")

    def MOD(j, which, kc=None):
        o = {"sh1": 0, "sc1": 16, "gt1": 32, "sh2": 48, "sc2": 64, "gt2": 80}[which]
        if kc is None:
            return modt[:, j, o:o + 16]
        return modt[:, j, o + kc:o + kc + 1]

    def w_view(w_l, c0, n):
        return w_l.rearrange("(kc p) n -> p kc n", p=128)[:, :, c0:c0 + n]

    def rsqrt(out, in_, addc, reads, writes):
        K.op("act", lambda e: e.activation(out=out, in_=in_, func=AF.Ln, bias=float(addc), scale=1.0),
             reads=reads, writes=writes)
        K.op("act", lambda e: e.activation(out=out, in_=out, func=AF.Exp, scale=-0.5), reads=writes, writes=writes)

    def norm_phase(l, xsrc_p, xsrc_s, Aa, shname, out_hl, ntp=16):
        with ExitStack() as ph:
            xt = [sb("xt%d" % i, [128, D], stack=ph) for i in range(2)]
            R_xt = [Res("xt0"), Res("xt1")]
            junk = sb("junk", [128, D], BF16, stack=ph)
            ssq = sb("ssq", [128, 2], stack=ph)
            R_junk, R_ssq = Res("junk"), Res("ssq")
            for i in list(range(ntp)) + [16]:
                np_ = 128 if i < 16 else 1
                j = 0 if i < 16 else 1
                col0 = 1 + 128 * i if i < 16 else SCOL
                src = xsrc_p(i) if i < 16 else xsrc_s
                s_ = i % 2
                xt_, Rx = xt[s_], R_xt[s_]
                K.dma("sp", xt_[0:np_, :], src, reads=[R_xbuf, R_x1buf, R_xfull, R_xh], writes=[Rx])
                K.op("act", lambda e, xt_=xt_, np_=np_, s_=s_: e.activation(
                    out=junk[0:np_, :], in_=xt_[0:np_, :], func=AF.Square, accum_out=ssq[0:np_, s_:s_ + 1]),
                    reads=[Rx], writes=[R_junk, R_ssq])
                rsqrt(ssq[0:np_, s_:s_ + 1], ssq[0:np_, s_:s_ + 1], D * EPS, [R_ssq], [R_ssq])
                K.op("dve", lambda e, xt_=xt_, np_=np_, s_=s_: e.tensor_scalar(
                    out=xt_[0:np_, :], in0=xt_[0:np_, :], scalar1=ssq[0:np_, s_:s_ + 1], scalar2=None,
                    op0=ALU.mult), reads=[R_ssq, Rx], writes=[Rx])
                for b in range(4):
                    bank = 4 * s_ + b
                    for jj in range(4):
                        kc = 4 * b + jj
                        K.op("pe", lambda e, bank=bank, jj=jj, kc=kc, xt_=xt_, np_=np_: e.transpose(
                            out=psb[bank][:, jj * 128:jj * 128 + np_], in_=xt_[0:np_, kc * 128:(kc + 1) * 128],
                            identity=ident[0:np_, 0:np_]), reads=[Rx, R_c], writes=[R_ps[bank]], sig=(jj == 3))
                    for jj in range(4):
                        kc = 4 * b + jj
                        K.op("act", lambda e, bank=bank, jj=jj, kc=kc, np_=np_, col0=col0, j=j: e.activation(
                            out=hT[:, kc, col0:col0 + np_], in_=psb[bank][:, jj * 128:jj * 128 + np_],
                            func=AF.Identity, scale=Aa[:, j, kc:kc + 1], bias=MOD(j, shname, kc)),
                            reads=[R_ps[bank], R_mod], writes=[R_hT], sig=(jj == 3 and (not out_hl or i < 15)))
                    if out_hl and i >= 15:
                        for jj in range(4):
                            kc = 4 * b + jj
                            K.op("act", lambda e, bank=bank, jj=jj, kc=kc, np_=np_, j=j: e.activation(
                                out=hl[:, j, kc:kc + 1], in_=psb[bank][:, jj * 128 + np_ - 1:jj * 128 + np_],
                                func=AF.Identity, scale=Aa[:, j, kc:kc + 1], bias=MOD(j, shname, kc)),
                                reads=[R_ps[bank], R_mod], writes=[R_hl], sig=(jj == 3))
            if out_hl:
                K.dma("sp", nsh_p[l], hl[:, 0, :], reads=[R_hl], writes=[Res('o')])
                K.dma("sp", nsh_s[l], hl[:, 1, :], reads=[R_hl], writes=[Res('o')])
            K.barrier()

    def adaln_phase(l):
        with ExitStack() as ph:
            wb = [sb("wb%d" % i, [128, 16, 512], BF16, stack=ph) for i in range(2)]
            R_wb = [Res("wb0"), Res("wb1")]
            tmp = sb("tmpa", [128, 16], stack=ph)
            shs = sb("shs", [128, 16], stack=ph)
            R_t = Res("tmpa")
            psM = psb[7]
            for og in range(24):
                s_ = og % 2
                K.dma("pool", wb[s_][:], w_view(ada_w[l], og * 512, 512), writes=[R_wb[s_]])
                for oc in range(4):
                    cc = og * 4 + oc
                    for kc in range(16):
                        K.op("pe", lambda e, s_=s_, oc=oc, cc=cc, kc=kc: e.matmul(
                            psM[:, 2 * cc:2 * cc + 2], lhsT=wb[s_][:, kc, oc * 128:(oc + 1) * 128],
                            rhs=sc_bf[:, kc, :], start=(kc == 0), stop=(kc == 15)),
                            reads=[R_wb[s_], R_c], writes=[R_ps[7]], sig=(kc == 15))
            for j in range(2):
                K.op("dve", lambda e, j=j: e.tensor_tensor(
                    out=modt[:, j, :], in0=psM[:, j:192:2], in1=V(l, "ada_b"), op=ALU.add),
                    reads=[R_ps[7], R_c], writes=[R_mod])
            for j in range(2):
                for Aa, scn, gn in ((A1, "sc1", "g1"), (A2, "sc2", "g2")):
                    K.op("dve", lambda e, j=j, scn=scn: e.tensor_scalar(
                        out=tmp[:], in0=MOD(j, scn), scalar1=1.0, scalar2=SQD, op0=ALU.add, op1=ALU.mult),
                        reads=[R_mod], writes=[R_t])
                    K.op("dve", lambda e, j=j, Aa=Aa, gn=gn: e.tensor_tensor(
                        out=Aa[:, j, :], in0=tmp[:], in1=V(l, gn), op=ALU.mult),
                        reads=[R_t, R_c], writes=[R_mod])
            K.dma("sp", shs[:], sshT_d[l], writes=[R_t])
            K.op("dve", lambda e: e.tensor_copy(out=hT[:, :, 2049], in_=shs[:]), reads=[R_t], writes=[R_hT])
            K.barrier()

    mixS = sb("mixS", [128, 16], BF16)
    R_mixS = Res("mixS")
    gk8 = sb("gk8", [128, NL])
    for l_ in range(NL):
        K.op("dve", lambda e, l_=l_: e.tensor_scalar(out=gk8[:, l_:l_ + 1], in0=V(l_, "gk"), scalar1=8.0,
                                                    scalar2=None, op0=ALU.mult), reads=[R_c], writes=[R_c])

    def proj_chunk(l, wslot, R_w, colbase, dstf, R_dst, pbanks, wsrc=None, kch=16, src_off=0, tiles=None):
        K.dma("pool", wslot[:], w_view(w_in[l] if wsrc is None else wsrc, colbase, 128), reads=[], writes=[R_w])
        for nt, (c0, n) in enumerate(NTILES if tiles is None else tiles):
            bank = pbanks[nt % 2]
            for kc in range(kch):
                K.op("pe", lambda e, bank=bank, kc=kc, c0=c0, n=n: e.matmul(
                    psb[bank][:, 0:n], lhsT=wslot[:, kc, :], rhs=hT[:, kc, c0:c0 + n],
                    start=(kc == 0), stop=(kc == kch - 1)),
                    reads=[R_w, R_hT], writes=[R_ps[bank]], sig=(kc == kch - 1))
            K.op("act", lambda e, bank=bank, c0=c0, n=n: e.copy(out=dstf[:, c0:c0 + n], in_=psb[bank][:, 0:n]),
                 reads=[R_ps[bank]], writes=[R_dst])

    def TT(out, in0, in1, op, reads, writes, eng="dve"):
        K.op(eng, lambda e: e.tensor_tensor(out=out, in0=in0, in1=in1, op=op), reads=reads, writes=writes)

    def TS(out, in0, s1, s2, op0, op1, reads, writes):
        if s2 is None:
            K.op("dve", lambda e: e.tensor_scalar(out=out, in0=in0, scalar1=s1, scalar2=None, op0=op0), reads=reads, writes=writes)
        else:
            K.op("dve", lambda e: e.tensor_scalar(out=out, in0=in0, scalar1=s1, scalar2=s2, op0=op0, op1=op1), reads=reads, writes=writes)

    def STT(out, in0, scalar, in1, op0, op1, reads, writes):
        K.op("dve", lambda e: e.scalar_tensor_tensor(out=out, in0=in0, scalar=scalar, in1=in1, op0=op0, op1=op1),
             reads=reads, writes=writes)

    def ACT(out, in_, func, reads, writes, bias=None, scale=1.0):
        if bias is None:
            K.op("act", lambda e: e.activation(out=out, in_=in_, func=func, scale=scale), reads=reads, writes=writes)
        else:
            K.op("act", lambda e: e.activation(out=out, in_=in_, func=func, bias=bias, scale=scale), reads=reads, writes=writes)

    def MM(out, lhsT, rhs, start, stop, reads, writes, sig=None):
        K.op("pe", lambda e: e.matmul(out, lhsT=lhsT, rhs=rhs, start=start, stop=stop), reads=reads, writes=writes,
             sig=(stop if sig is None else sig))

    def TR(out, in_, idn, reads, writes):
        K.op("pe", lambda e: e.transpose(out=out, in_=in_, identity=idn), reads=reads, writes=writes)

    LGROUPS = [(1, 512), (513, 512), (1025, 512), (1537, 512), (SCOL, 1)]
    LNW = -math.exp(-0.5)

    def rwkv_phase(l):
        with ExitStack() as ph:
            lows_da = sb("lows_da", [128, NCOL], BF16, stack=ph)
            lows_g = sb("lows_g", [128, NCOL], BF16, stack=ph)
            lows_gv = sb("lows_gv", [64, NCOL], BF16, stack=ph)
            W2da = sb("W2da", [128, 1024], BF16, stack=ph)
            W2g = sb("W2g", [128, 1024], BF16, stack=ph)
            W2gv = sb("W2gv", [64, 1024], BF16, stack=ph)
            R_lows, R_W2 = Res("lows"), Res("W2")
            K.dma("pool", W2da[0:64, :], dw2[l], writes=[R_W2])
            K.dma("pool", W2da[64:128, :], aw2[l], writes=[R_W2])
            K.dma("pool", W2g[:, :], gw2[l][0:128, :], writes=[R_W2])
            K.dma("pool", W2gv[0:32, :], gw2[l][128:160, :], writes=[R_W2])
            if l > 0:
                K.dma("pool", W2gv[32:64, :], vw2[l - 1], writes=[R_W2])
            with ExitStack() as p0:
                Wl = sb("Wl", [128, 16, 320], stack=p0)
                Wa = sb("Wa", [128, 16, 320], BF16, stack=p0)
                Wb = sb("Wb", [128, 16, 320], BF16, stack=p0)
                R_Wl, R_Wab = Res("Wl"), Res("Wab")
                if l == 0:
                    K.op("dve", lambda e: e.memset(Wl[:, :, 288:320], 0.0), writes=[R_Wl])
                for src, c0_, w_ in ((dw1[l], 0, 64), (aw1[l], 64, 64), (gw1[l], 128, 160)) + (((vw1[l - 1], 288, 32),) if l > 0 else ()):
                    K.dma("sp", Wl[:, :, c0_:c0_ + w_], src.rearrange("(kc p) n -> p kc n", p=128), writes=[R_Wl])
                for mun, c0_, w_ in (("mu_w", 0, 64), ("mu_a", 64, 64), ("mu_g", 128, 160), ("vmu", 288, 32)):
                    TT(Wb[:, :, c0_:c0_ + w_], Wl[:, :, c0_:c0_ + w_], V(l, mun).unsqueeze(2).to_broadcast([128, 16, w_]),
                       ALU.mult, [R_Wl, R_c], [R_Wab])
                TT(Wa[:], Wl[:], Wb[:], ALU.subtract, [R_Wl, R_Wab], [R_Wab])
                for gi, (c0, n) in enumerate(LGROUPS):
                    for mi, (m0, mw, dst) in enumerate(((0, 128, lows_da), (128, 128, lows_g), (256, 64, lows_gv))):
                        bank = (gi * 3 + mi) % 2
                        for kc in range(16):
                            MM(psb[bank][0:mw, 0:n], Wa[:, kc, m0:m0 + mw], hT[:, kc, c0:c0 + n], kc == 0, False,
                               [R_Wab, R_hT], [R_ps[bank]], sig=False)
                        for kc in range(16):
                            MM(psb[bank][0:mw, 0:n], Wb[:, kc, m0:m0 + mw], hT[:, kc, c0 - 1:c0 - 1 + n], False, kc == 15,
                               [R_Wab, R_hT], [R_ps[bank]])
                        if mi == 0:
                            ACT(dst[0:64, c0:c0 + n], psb[bank][0:64, 0:n], AF.Tanh, [R_ps[bank]], [R_lows])
                            ACT(dst[64:128, c0:c0 + n], psb[bank][64:128, 0:n], AF.Identity, [R_ps[bank]], [R_lows])
                        elif mi == 1:
                            ACT(dst[:, c0:c0 + n], psb[bank][:, 0:n], AF.Sigmoid, [R_ps[bank]], [R_lows])
                        else:
                            ACT(dst[0:32, c0:c0 + n], psb[bank][0:32, 0:n], AF.Sigmoid, [R_ps[bank]], [R_lows])
                            ACT(dst[32:64, c0:c0 + n], psb[bank][32:64, 0:n], AF.Identity, [R_ps[bank]], [R_lows])
                K.barrier()
            names = "rr kk2 vv lw kn bb gg t1 t2 t3".split()
            B = {n: sb("rw_" + n, [128, NCOL], stack=ph) for n in names}
            RB = {n: Res("rw_" + n) for n in names}
            wsl = [sb("rwsl%d" % i, [128, 16, 128], BF16, stack=ph) for i in range(2)]
            R_wsl = [Res("rwsl0"), Res("rwsl1")]
            mixc = sb("mixc2", [128, 2049], BF16, stack=ph)
            R_mixc = Res("mixc2")
            S0T = sb("S0T", [128, 64], stack=ph)
            SsT = sb("SsT", [128, 64], stack=ph)
            Sin = sb("Sin", [64, 2, 64], stack=ph)
            Sout = sb("Sout", [64, 2, 64], stack=ph)
            gbt = sb("gbt", [128, 128], stack=ph)
            bbt = sb("bbt", [128, 128], stack=ph)
            R_S0T, R_SsT, R_Sin, R_Sout, R_gb = Res("S0T"), Res("SsT"), Res("Sin"), Res("Sout"), Res("gbt")
            cn = "lp P Pi Pp Rt At Bt Kt Bh Kh rkp".split()
            CB = {n: sb("ck_" + n, [128, 128], stack=ph) for n in cn}
            RC = {n: Res("ck_" + n) for n in cn}
            tmn = "Bhm Khm Vtm yf yn".split()
            TM = {n: sb("tm_" + n, [128, 128], stack=ph) for n in tmn}
            RT = {n: Res("tm_" + n) for n in tmn}
            hn = "Aab Arb Aak Ark X XT X2 XT2 W".split()
            hn = hn + ["Wb"]
            HB = [{n: sb("h%d_%s" % (hh, n), [128, 64 if n in ("W", "Wb") else 128],
                         F32R if n in ("Aab", "X", "XT", "X2", "XT2", "Wb") else F32, stack=ph) for n in hn} for hh in range(2)]
            RH = [{n: Res("h%d_%s" % (hh, n)) for n in hn} for hh in range(2)]
            stat = sb("stat", [128, 2, 6], stack=ph)
            mv = sb("mv", [128, 2, 2], stack=ph)
            rk = sb("rk", [128, 2], stack=ph)
            R_stat, R_mv, R_rk = Res("stat"), Res("mv"), Res("rk")

            def chunk(c, cs, C, St, R_St, zero_state, mcol):
                sl = slice(cs, cs + C)
                lp, P, Pi, Pp = CB["lp"], CB["P"], CB["Pi"], CB["Pp"]
                K.op("dve", lambda e: e.tensor_tensor_scan(out=lp[:, 0:C], data0=ones_f[:, 0:C], data1=B["lw"][:, sl],
                                                           initial=0.0, op0=ALU.mult, op1=ALU.add),
                     reads=[RB["lw"], R_c], writes=[RC["lp"]])
                ACT(P[:, 0:C], lp[:, 0:C], AF.Exp, [RC["lp"]], [RC["P"]])
                ACT(Pi[:, 0:C], lp[:, 0:C], AF.Exp, [RC["lp"]], [RC["Pi"]], scale=-1.0)
                TT(Pp[:, 0:C], lp[:, 0:C], B["lw"][:, sl], ALU.subtract, [RC["lp"], RB["lw"]], [RC["Pp"]])
                ACT(Pp[:, 0:C], Pp[:, 0:C], AF.Exp, [RC["Pp"]], [RC["Pp"]])
                TT(CB["Rt"][:, 0:C], B["rr"][:, sl], P[:, 0:C], ALU.mult, [RB["rr"], RC["P"]], [RC["Rt"]])
                STT(CB["At"][:, 0:C], B["kn"][:, sl], -1.0, Pp[:, 0:C], ALU.mult, ALU.mult, [RB["kn"], RC["Pp"]], [RC["At"]])
                TT(CB["Bt"][:, 0:C], B["bb"][:, sl], Pi[:, 0:C], ALU.mult, [RB["bb"], RC["Pi"]], [RC["Bt"]])
                TT(CB["Kt"][:, 0:C], B["kk2"][:, sl], Pi[:, 0:C], ALU.mult, [RB["kk2"], RC["Pi"]], [RC["Kt"]])
                TS(CB["Bh"][:, 0:C], CB["Bt"][:, 0:C], P[:, C - 1:C], None, ALU.mult, None, [RC["Bt"], RC["P"]], [RC["Bh"]])
                TS(CB["Kh"][:, 0:C], CB["Kt"][:, 0:C], P[:, C - 1:C], None, ALU.mult, None, [RC["Kt"], RC["P"]], [RC["Kh"]])
                STT(CB["rkp"][:, 0:C], B["rr"][:, sl], V(l, "r_k", c), B["kk2"][:, sl], ALU.mult, ALU.mult,
                    [RB["rr"], RB["kk2"], R_c], [RC["rkp"]])
                for srcb, Rsrc, dstn in ((CB["Bh"][:, 0:C], RC["Bh"], "Bhm"), (CB["Kh"][:, 0:C], RC["Kh"], "Khm"),
                                         (B["vv"][:, sl], RB["vv"], "Vtm")):
                    TR(psb[3][0:C, 0:128], srcb, ident[:], [Rsrc, R_c], [R_ps[3]])
                    ACT(TM[dstn][0:C, :], psb[3][0:C, 0:128], AF.Identity, [R_ps[3]], [RT[dstn]])
                MM(psb[3][0:C, 0:2], CB["rkp"][:, 0:C], blockones[:, 0:128:64], True, True, [RC["rkp"], R_c], [R_ps[3]])
                ACT(rk[0:C, :], psb[3][0:C, 0:2], AF.Identity, [R_ps[3]], [R_rk])
                for hh in range(2):
                    pb = 64 * hh
                    bA, bB = 4 + hh, 6 + hh
                    H, RHh = HB[hh], RH[hh]
                    Bt_, Kt_, At_, Rt_ = CB["Bt"][pb:pb + 64, 0:C], CB["Kt"][pb:pb + 64, 0:C], CB["At"][pb:pb + 64, 0:C], CB["Rt"][pb:pb + 64, 0:C]
                    rd = [RC["Bt"], RC["Kt"], RC["At"], RC["Rt"]]
                    MM(psb[bA][0:C, 0:C], Bt_, At_, True, True, rd, [R_ps[bA]], sig=False)
                    MM(psb[bA][0:C, 128:128 + C], Bt_, Rt_, True, True, rd, [R_ps[bA]], sig=False)
                    MM(psb[bA][0:C, 256:256 + C], Kt_, At_, True, True, rd, [R_ps[bA]], sig=False)
                    MM(psb[bA][0:C, 384:384 + C], Kt_, Rt_, True, True, rd, [R_ps[bA]])
                    MM(psb[bB][0:C, 0:C], At_, Bt_, True, True, rd, [R_ps[bB]])
                    TT(H["Aab"][0:C, 0:C], psb[bA][0:C, 0:C], m_lt[0:C, 0:C], ALU.mult, [R_ps[bA], R_c], [RHh["Aab"]])
                    TT(H["Arb"][0:C, 0:C], psb[bA][0:C, 128:128 + C], m_le[0:C, 0:C], ALU.mult, [R_ps[bA], R_c], [RHh["Arb"]])
                    TT(H["Aak"][0:C, 0:C], psb[bA][0:C, 256:256 + C], m_lt[0:C, 0:C], ALU.mult, [R_ps[bA], R_c], [RHh["Aak"]])
                    TT(H["Ark"][0:C, 0:C], psb[bA][0:C, 384:384 + C], m_le[0:C, 0:C], ALU.mult, [R_ps[bA], R_c], [RHh["Ark"]])
                    TT(H["X"][0:C, 0:C], psb[bB][0:C, 0:C], m_gt[0:C, 0:C], ALU.mult, [R_ps[bB], R_c], [RHh["X"]])
                for hh in range(2):
                    pb = 64 * hh
                    bB = 6 + hh
                    H, RHh = HB[hh], RH[hh]
                    At_ = CB["At"][pb:pb + 64, 0:C]
                    if not zero_state:
                        MM(psb[bB][0:C, 128:192], At_, St[pb:pb + 64, :], True, False, [RC["At"], R_St], [R_ps[bB]], sig=False)
                    MM(psb[bB][0:C, 128:192], H["Aak"][0:C, 0:C], TM["Vtm"][0:C, pb:pb + 64], zero_state, True,
                       [RHh["Aak"], RT["Vtm"]], [R_ps[bB]])
                    ACT(H["W"][0:C, :], psb[bB][0:C, 128:192], AF.Identity, [R_ps[bB]], [RHh["W"]])
                    if C > 1:
                        ACT(H["Wb"][0:C, :], psb[bB][0:C, 128:192], AF.Identity, [R_ps[bB]], [RHh["Wb"]])
                nlev = 0
                while (1 << nlev) < C:
                    nlev += 1
                curX = ["X", "Aab"]
                nxtX = ["X2", "XT2"]
                for lev in range(nlev):
                    lastlev = (lev == nlev - 1)
                    for hh in range(2):
                        bB = 6 + hh
                        H, RHh = HB[hh], RH[hh]
                        Xn, XTn = curX
                        MM(psb[bB][0:C, 128:192], H[XTn][0:C, 0:C], H["Wb"][0:C, :], True, True, [RHh[XTn], RHh["Wb"]], [R_ps[bB]])
                        if not lastlev:
                            MM(psb[bB][0:C, 256:256 + C], H[XTn][0:C, 0:C], H[Xn][0:C, 0:C], True, True, [RHh[XTn], RHh[Xn]], [R_ps[bB]], sig=False)
                            MM(psb[bB][0:C, 384:384 + C], H[Xn][0:C, 0:C], H[XTn][0:C, 0:C], True, True, [RHh[XTn], RHh[Xn]], [R_ps[bB]])
                        if not lastlev:
                            TT(H["Wb"][0:C, :], H["W"][0:C, :], psb[bB][0:C, 128:192], ALU.add, [RHh["W"], R_ps[bB]], [RHh["Wb"]])
                        TT(H["W"][0:C, :], H["W"][0:C, :], psb[bB][0:C, 128:192], ALU.add, [RHh["W"], R_ps[bB]], [RHh["W"]])
                        if not lastlev:
                            ACT(H[nxtX[0]][0:C, 0:C], psb[bB][0:C, 256:256 + C], AF.Identity, [R_ps[bB]], [RHh[nxtX[0]]])
                            ACT(H[nxtX[1]][0:C, 0:C], psb[bB][0:C, 384:384 + C], AF.Identity, [R_ps[bB]], [RHh[nxtX[1]]])
                    if lev == 0:
                        curX, nxtX = ["X2", "XT2"], ["X", "XT"]
                    else:
                        curX, nxtX = nxtX, curX
                for hh in range(2):
                    pb = 64 * hh
                    H, RHh = HB[hh], RH[hh]
                    osl = psb[hh][0:C, 0:64]
                    if not zero_state:
                        MM(osl, CB["Rt"][pb:pb + 64, 0:C], St[pb:pb + 64, :], True, False, [RC["Rt"], R_St], [R_ps[hh]], sig=False)
                    MM(osl, H["Arb"][0:C, 0:C], H["W"][0:C, :], zero_state, False, [RHh["Arb"], RHh["W"]], [R_ps[hh]], sig=False)
                    MM(osl, H["Ark"][0:C, 0:C], TM["Vtm"][0:C, pb:pb + 64], False, True, [RHh["Ark"], RT["Vtm"]], [R_ps[hh]])
                for hh in range(2):
                    pb = 64 * hh
                    H, RHh = HB[hh], RH[hh]
                    MM(psb[3][pb:pb + 64, 0:64], TM["Bhm"][0:C, pb:pb + 64], H["W"][0:C, :], True, False,
                       [RT["Bhm"], RHh["W"]], [R_ps[3]], sig=False)
                    MM(psb[3][pb:pb + 64, 0:64], TM["Khm"][0:C, pb:pb + 64], TM["Vtm"][0:C, pb:pb + 64], False, True,
                       [RT["Khm"], RT["Vtm"]], [R_ps[3]])
                if zero_state:
                    ACT(St[:, :], psb[3][:, 0:64], AF.Identity, [R_ps[3]], [R_St])
                else:
                    STT(St[:, :], St[:, :], CB["P"][:, C - 1:C], psb[3][:, 0:64], ALU.mult, ALU.add, [R_St, RC["P"], R_ps[3]], [R_St])
                for hh in range(2):
                    K.op("dve", lambda e, hh=hh: e.bn_stats(out=stat[0:C, hh, :], in_=psb[hh][0:C, 0:64]),
                         reads=[R_ps[hh]], writes=[R_stat])
                    K.op("dve", lambda e, hh=hh: e.bn_aggr(out=mv[0:C, hh, :], in_=stat[0:C, hh, :]), reads=[R_stat], writes=[R_mv])
                rsqrt(mv[0:C, :, 1], mv[0:C, :, 1], GN_EPS, [R_mv], [R_mv])
                for hh in range(2):
                    TS(TM["yn"][0:C, hh * 64:(hh + 1) * 64], psb[hh][0:C, 0:64], mv[0:C, hh, 0:1], mv[0:C, hh, 1:2],
                       ALU.subtract, ALU.mult, [R_ps[hh], R_mv], [RT["yn"]])
                TT(TM["yn"][0:C, :], TM["yn"][0:C, :], gbt[0:C, :], ALU.mult, [RT["yn"], R_gb], [RT["yn"]])
                TT(TM["yn"][0:C, :], TM["yn"][0:C, :], bbt[0:C, :], ALU.add, [RT["yn"], R_gb], [RT["yn"]])
                for hh in range(2):
                    STT(TM["yf"][0:C, hh * 64:(hh + 1) * 64], TM["Vtm"][0:C, hh * 64:(hh + 1) * 64], rk[0:C, hh:hh + 1],
                        TM["yn"][0:C, hh * 64:(hh + 1) * 64], ALU.mult, ALU.add, [RT["Vtm"], R_rk, RT["yn"]], [RT["yf"]])
                TR(psb[3][:, 64:64 + C], TM["yf"][0:C, :], ident[0:C, 0:C], [RT["yf"], R_c], [R_ps[3]])
                TT(mixc[:, mcol:mcol + C], psb[3][:, 64:64 + C], B["gg"][:, sl], ALU.mult, [R_ps[3], RB["gg"]], [R_mixc])

            for c in range(rw_c):
                mine = c < 4
                lo = 1 if mine else 2049
                lg = LGROUPS if mine else LGROUPS[4:]
                tl = NTILES if mine else NTILES[4:]
                for wi, (cb, dstn, mun) in enumerate(((3072, "rr", "mu_r"), (4096, "kk2", "mu_k"), (5120, "vv", "mu_v"))):
                    proj_chunk(l, wsl[wi % 2], R_wsl[wi % 2], cb + c * 128, B["t1"], RB["t1"], (0, 1), tiles=tl)
                    TT(B["t2"][:, lo - 1:NCOL - 1], B["t1"][:, lo - 1:NCOL - 1], B["t1"][:, lo:NCOL], ALU.subtract, [RB["t1"]], [RB["t2"]])
                    STT(B[dstn][:, lo:NCOL], B["t2"][:, lo - 1:NCOL - 1], V(l, mun, c), B["t1"][:, lo:NCOL], ALU.mult, ALU.add,
                        [RB["t1"], RB["t2"], R_c], [RB[dstn]])
                for gi, (c0, n) in enumerate(lg):
                    cs_ = slice(c0, c0 + n)
                    MM(psb[0][:, 0:n], W2da[0:64, c * 128:(c + 1) * 128], lows_da[0:64, cs_], True, True, [R_W2, R_lows], [R_ps[0]])
                    ACT(B["lw"][:, cs_], psb[0][:, 0:n], AF.Sigmoid, [R_ps[0], R_c], [RB["lw"]], bias=V(l, "w0", c))
                    MM(psb[1][:, 0:n], W2da[64:128, c * 128:(c + 1) * 128], lows_da[64:128, cs_], True, True, [R_W2, R_lows], [R_ps[1]])
                    ACT(B["t1"][:, cs_], psb[1][:, 0:n], AF.Sigmoid, [R_ps[1], R_c], [RB["t1"]], bias=V(l, "a0", c))
                    MM(psb[0][:, 0:n], W2g[:, c * 128:(c + 1) * 128], lows_g[:, cs_], True, False, [R_W2, R_lows], [R_ps[0]], sig=False)
                    MM(psb[0][:, 0:n], W2gv[0:32, c * 128:(c + 1) * 128], lows_gv[0:32, cs_], False, True, [R_W2, R_lows], [R_ps[0]])
                    ACT(B["gg"][:, cs_], psb[0][:, 0:n], AF.Identity, [R_ps[0]], [RB["gg"]])
                    if l > 0:
                        MM(psb[1][:, 0:n], W2gv[32:64, c * 128:(c + 1) * 128], lows_gv[32:64, cs_], True, True, [R_W2, R_lows], [R_ps[1]])
                        ACT(B["t3"][:, cs_], psb[1][:, 0:n], AF.Sigmoid, [R_ps[1], R_c], [RB["t3"]], bias=V(l, "v0", c))
                TS(B["lw"][:, lo:NCOL], B["lw"][:, lo:NCOL], LNW, None, ALU.mult, None, [RB["lw"]], [RB["lw"]])
                if l == 0:
                    K.dma("sp", vfirst_d[c], B["vv"][:], reads=[RB["vv"]], writes=[R_vfirst])
                else:
                    K.dma("sp", B["t2"][:], vfirst_d[c], reads=[R_vfirst], writes=[RB["t2"]])
                    TT(B["t2"][:, lo:NCOL], B["t2"][:, lo:NCOL], B["vv"][:, lo:NCOL], ALU.subtract, [RB["t2"], RB["vv"]], [RB["t2"]])
                    TT(B["t2"][:, lo:NCOL], B["t2"][:, lo:NCOL], B["t3"][:, lo:NCOL], ALU.mult, [RB["t2"], RB["t3"]], [RB["t2"]])
                    TT(B["vv"][:, lo:NCOL], B["vv"][:, lo:NCOL], B["t2"][:, lo:NCOL], ALU.add, [RB["t2"], RB["vv"]], [RB["vv"]])
                TS(B["kn"][:, lo:NCOL], B["kk2"][:, lo:NCOL], V(l, "k_k", c), None, ALU.mult, None, [RB["kk2"], R_c], [RB["kn"]])
                ACT(B["t2"][:, lo:NCOL], B["kn"][:, lo:NCOL], AF.Square, [RB["kn"]], [RB["t2"]])
                for gi, (c0, n) in enumerate(lg):
                    MM(psb[2][:, 0:n], blockones[:], B["t2"][:, c0:c0 + n], True, True, [R_c, RB["t2"]], [R_ps[2]])
                    rsqrt(B["t3"][:, c0:c0 + n], psb[2][:, 0:n], 1e-24, [R_ps[2]], [RB["t3"]])
                TT(B["kn"][:, lo:NCOL], B["kn"][:, lo:NCOL], B["t3"][:, lo:NCOL], ALU.mult, [RB["kn"], RB["t3"]], [RB["kn"]])
                TT(B["bb"][:, lo:NCOL], B["kn"][:, lo:NCOL], B["t1"][:, lo:NCOL], ALU.mult, [RB["kn"], RB["t1"]], [RB["bb"]])
                TS(B["t2"][:, lo:NCOL], B["t1"][:, lo:NCOL], V(l, "k_a", c), V(l, "k_a", c), ALU.mult, ALU.subtract, [RB["t1"], R_c], [RB["t2"]])
                STT(B["kk2"][:, lo:NCOL], B["t2"][:, lo:NCOL], 1.0, B["kk2"][:, lo:NCOL], ALU.add, ALU.mult, [RB["t2"], RB["kk2"]], [RB["kk2"]])
                for src_, dst_ in (("lnx_g", gbt), ("lnx_b", bbt)):
                    K.op("dve", lambda e, src_=src_, c=c: e.tensor_copy(out=bct[:], in_=V(l, src_, c).to_broadcast([128, 128])),
                         reads=[R_c], writes=[R_bct])
                    MM(psb[7][:, 0:128], bct[:], ident[:], True, True, [R_bct, R_c], [R_ps[7]])
                    ACT(dst_[:, :], psb[7][:, 0:128], AF.Identity, [R_ps[7]], [R_gb])
                for i in range(rw_nch if mine else 0):
                    chunk(c, 1 + 128 * i, 128, S0T, R_S0T, i == 0, 128 * i)
                for hh in range(2 if mine else 0):
                    pb = 64 * hh
                    TR(psb[hh][0:64, 0:64], S0T[pb:pb + 64, :], ident[pb:pb + 64, pb:pb + 64], [R_S0T, R_c], [R_ps[hh]])
                    ACT(Sout[:, hh, :], psb[hh][0:64, 0:64], AF.Identity, [R_ps[hh]], [R_Sout])
                if mine:
                    K.dma("sp", nwkv_p[l][2 * c:2 * c + 2].rearrange("h i j -> i h j"), Sout[:], reads=[R_Sout], writes=[Res('o')])
                K.dma("sp", Sin[:], swkv_d[l][2 * c:2 * c + 2].rearrange("h i j -> i h j"), writes=[R_Sin])
                for hh in range(2):
                    pb = 64 * hh
                    MM(psb[3][pb:pb + 64, 256:320], Sin[:, hh, :], ident[0:64, 0:64], True, True, [R_Sin, R_c], [R_ps[3]])
                ACT(SsT[:, :], psb[3][:, 256:320], AF.Identity, [R_ps[3]], [R_SsT])
                if rw_sample:
                    chunk(c, SCOL, 1, SsT, R_SsT, False, 2048)
                for hh in range(2):
                    pb = 64 * hh
                    TR(psb[hh][0:64, 0:64], SsT[pb:pb + 64, :], ident[pb:pb + 64, pb:pb + 64], [R_SsT, R_c], [R_ps[hh]])
                    ACT(Sout[:, hh, :], psb[hh][0:64, 0:64], AF.Identity, [R_ps[hh]], [R_Sout])
                K.dma("sp", nwkv_s[l][2 * c:2 * c + 2].rearrange("h i j -> i h j"), Sout[:], reads=[R_Sout], writes=[Res('o')])
                if mine:
                    for hf in range(2):
                        K.dma("sp", mixh[hf][(4 + c) * 128:(5 + c) * 128, :], mixc[:, hf * TH:(hf + 1) * TH],
                              reads=[R_mixc, R_mixg], writes=[R_mixh])
                K.op("dve", lambda e, c=c: e.tensor_copy(out=mixS[:, 8 + c:9 + c], in_=mixc[:, 2048:2049]), reads=[R_mixc], writes=[R_mixS])
            K.barrier()

    def att_phase(l):
        with ExitStack() as ph:
            qrow = sb("qrow", [1, 1024], stack=ph)
            krow = sb("krow", [1, 1024], stack=ph)
            vrow = sb("vrow", [1, 1024], stack=ph)
            ph1 = ExitStack()
            wsl = [sb("wsl%d" % i, [128, 16, 128], BF16, stack=ph1) for i in range(2)]
            R_wsl = [Res("wsl0"), Res("wsl1")]
            qf = sb("qf", [128, NCOL], stack=ph1)
            kf = sb("kf", [128, NCOL], stack=ph1)
            vf = sb("vf", [128, NCOL], stack=ph1)
            sqf = sb("sqf", [128, NCOL], stack=ph1)
            rs = sb("rs", [128, NCOL], stack=ph1)
            qn = sb("qn", [128, NCOL], BF16, stack=ph1)
            kn = sb("kn", [128, NCOL], BF16, stack=ph1)
            tm = sb("tm", [128, 16, 128], stack=ph1)
            Vp = sb("Vp", [128, 16, 2, 65], BF16, stack=ph1)
            strip = [sb("strip%d" % i, [128, 2048], stack=ph1) for i in range(2)]
            pexp = [sb("pexp%d" % i, [128, 512], stack=ph1) for i in range(2)]
            PT = [sb("PT%d" % i, [128, 512], BF16, stack=ph1) for i in range(2)]
            att_tm = sb("att_tm", [128, 128], stack=ph1)
            rden = sb("rden", [128, 2], stack=ph1)
            mixc = sb("mixc", [128, 2048], BF16, stack=ph1)
            qcol = sb("qcol", [128, 1], stack=ph1)
            R_qf, R_kf, R_vf, R_sqf, R_rs, R_qn, R_kn, R_tm, R_Vp = [Res(n) for n in
                                                                      "qf kf vf sqf rs qn kn tm Vp".split()]
            R_strip = [Res("strip0"), Res("strip1")]
            R_pexp = [Res("pexp0"), Res("pexp1")]
            R_PT = [Res("PT0"), Res("PT1")]
            R_att, R_rden, R_mixc, R_rows, R_qcol = Res("att_tm"), Res("rden"), Res("mixc"), Res("rows"), Res("qcol")
            K.op("dve", lambda e: e.memset(Vp[:], 1.0), writes=[R_Vp])
            sidx = 0
            for c in range(8):
                mine = c < 4
                tl = NTILES if mine else NTILES[4:]
                ca = slice(0, NCOL) if mine else slice(2048, NCOL)
                proj_chunk(l, wsl[0], R_wsl[0], c * 128, qf, R_qf, (0, 1), tiles=tl)
                proj_chunk(l, wsl[1], R_wsl[1], 1024 + c * 128, kf, R_kf, (0, 1), tiles=tl)
                proj_chunk(l, wsl[0], R_wsl[0], 2048 + c * 128, vf, R_vf, (0, 1), tiles=tl)
                for Xf, R_X, isq in ((qf, R_qf, True), (kf, R_kf, False)):
                    K.op("act", lambda e, Xf=Xf, ca=ca: e.activation(out=sqf[:, ca], in_=Xf[:, ca], func=AF.Square),
                         reads=[R_X], writes=[R_sqf])
                    for nt, (c0, n) in enumerate(tl):
                        K.op("pe", lambda e, c0=c0, n=n: e.matmul(psb[2][:, 0:n], lhsT=blockones[:], rhs=sqf[:, c0:c0 + n],
                                                                  start=True, stop=True),
                             reads=[R_c, R_sqf], writes=[R_ps[2]])
                        rsqrt(rs[:, c0:c0 + n], psb[2][:, 0:n], 64 * EPS, [R_ps[2]], [R_rs])
                    if isq:
                        K.op("dve", lambda e, ca=ca: e.scalar_tensor_tensor(out=qn[:, ca], in0=qf[:, ca], scalar=V(l, "gq"), in1=rs[:, ca],
                                                                     op0=ALU.mult, op1=ALU.mult),
                             reads=[R_qf, R_rs, R_c], writes=[R_qn])
                        K.op("dve", lambda e: e.scalar_tensor_tensor(
                            out=qcol[:], in0=qf[:, SCOL:SCOL + 1], scalar=V(l, "gq"), in1=rs[:, SCOL:SCOL + 1],
                            op0=ALU.mult, op1=ALU.mult), reads=[R_qf, R_rs, R_c], writes=[R_qcol])
                    else:
                        K.op("dve", lambda e, ca=ca: e.scalar_tensor_tensor(out=kf[:, ca], in0=kf[:, ca], scalar=gk8[:, l:l + 1],
                                                                     in1=rs[:, ca], op0=ALU.mult, op1=ALU.mult),
                             reads=[R_kf, R_rs, R_c], writes=[R_kf])
                        K.op("dve", lambda e, ca=ca: e.tensor_copy(out=kn[:, ca], in_=kf[:, ca]), reads=[R_kf], writes=[R_kn])
                for Xf, R_X, dst, srow in ((kf, R_kf, nk_p, krow), (vf, R_vf, nv_p, vrow)):
                    for g in range(4 if mine else 0):
                        for jj in range(4):
                            i_ = 4 * g + jj
                            K.op("pe", lambda e, Xf=Xf, i_=i_, jj=jj: e.transpose(
                                out=psb[3][:, jj * 128:(jj + 1) * 128], in_=Xf[:, 1 + 128 * i_:1 + 128 * (i_ + 1)],
                                identity=ident[:]), reads=[R_X, R_c], writes=[R_ps[3]], sig=(jj == 3))
                        K.op("act", lambda e, g=g: e.copy(
                            out=tm[:, 4 * g:4 * g + 4, :].rearrange("p a b -> p (a b)"), in_=psb[3][:, :]),
                            reads=[R_ps[3]], writes=[R_tm])
                    if mine:
                        K.dma("sp", dst[l].rearrange("(i p) f -> p i f", p=128)[:, :, c * 128:(c + 1) * 128], tm[:],
                              reads=[R_tm], writes=[Res('o')])
                    if Xf is vf and mine:
                        K.op("dve", lambda e: e.tensor_copy(
                            out=Vp[:, :, :, 0:64], in_=tm[:].rearrange("p i (h e) -> p i h e", h=2)),
                            reads=[R_tm], writes=[R_Vp])
                    K.op("pe", lambda e, Xf=Xf: e.transpose(out=psb[3][0:1, 0:128], in_=Xf[:, SCOL:SCOL + 1],
                                                           identity=ident[:]),
                         reads=[R_X, R_c], writes=[R_ps[3]])
                    K.op("act", lambda e, srow=srow, c=c: e.copy(out=srow[0:1, c * 128:(c + 1) * 128], in_=psb[3][0:1, 0:128]),
                         reads=[R_ps[3]], writes=[R_rows])
                K.op("pe", lambda e: e.transpose(out=psb[3][0:1, 0:128], in_=qcol[:, 0:1], identity=ident[:]),
                     reads=[R_qcol, R_c], writes=[R_ps[3]])
                K.op("act", lambda e, c=c: e.copy(out=qrow[0:1, c * 128:(c + 1) * 128], in_=psb[3][0:1, 0:128]),
                     reads=[R_ps[3]], writes=[R_rows])
                for hh in range(2 if mine else 0):
                    K.dma("sp", strip[hh][:], estrip[2 * c + hh], reads=[R_estrip], writes=[R_strip[hh]])
                for qi in range(16 if mine else 0):
                    for hh in range(2):
                        pb = 64 * hh
                        nkb = qi + 1
                        ob = 6 + hh
                        first = True
                        for g0 in range(0, nkb, 4):
                            kbs = [qi - g0 - s_ for s_ in range(4) if qi - g0 - s_ >= 0]
                            n = 128 * len(kbs)
                            sl = sidx % 2
                            sidx += 1
                            bank = 4 + sl
                            for s_, kb in enumerate(kbs):
                                K.op("pe", lambda e, bank=bank, s_=s_, kb=kb, pb=pb, qi=qi: e.matmul(
                                    psb[bank][:, s_ * 128:(s_ + 1) * 128],
                                    lhsT=kn[pb:pb + 64, 1 + 128 * kb:1 + 128 * (kb + 1)],
                                    rhs=qn[pb:pb + 64, 1 + 128 * qi:1 + 128 * (qi + 1)], start=True, stop=True),
                                    reads=[R_kn, R_qn], writes=[R_ps[bank]], sig=(s_ == len(kbs) - 1))
                            K.op("act", lambda e, bank=bank, sl=sl, n=n: e.activation(
                                out=pexp[sl][:, 0:n], in_=psb[bank][:, 0:n], func=AF.Exp),
                                reads=[R_ps[bank]], writes=[R_pexp[sl]])
                            J0 = 128 * g0
                            K.op("dve", lambda e, sl=sl, n=n, J0=J0, hh=hh: e.tensor_tensor(
                                out=PT[sl][:, 0:n], in0=pexp[sl][:, 0:n], in1=strip[hh][:, J0:J0 + n], op=ALU.mult),
                                reads=[R_pexp[sl], R_strip[hh]], writes=[R_PT[sl]])
                            for s_, kb in enumerate(kbs):
                                last = (g0 + 4 >= nkb) and (s_ == len(kbs) - 1)
                                K.op("pe", lambda e, ob=ob, sl=sl, s_=s_, kb=kb, hh=hh, first=first, last=last: e.matmul(
                                    psb[ob][:, 0:65], lhsT=PT[sl][:, s_ * 128:(s_ + 1) * 128], rhs=Vp[:, kb, hh, :],
                                    start=first, stop=last),
                                    reads=[R_PT[sl], R_Vp], writes=[R_ps[ob]], sig=(s_ == len(kbs) - 1))
                                first = False
                        K.op("dve", lambda e, ob=ob, hh=hh: e.reciprocal(out=rden[:, hh:hh + 1], in_=psb[ob][:, 64:65]),
                             reads=[R_ps[ob]], writes=[R_rden])
                        K.op("dve", lambda e, ob=ob, hh=hh: e.tensor_scalar(
                            out=att_tm[:, hh * 64:(hh + 1) * 64], in0=psb[ob][:, 0:64], scalar1=rden[:, hh:hh + 1],
                            scalar2=None, op0=ALU.mult), reads=[R_ps[ob], R_rden], writes=[R_att])
                    K.op("pe", lambda e: e.transpose(out=psb[3][:, 0:128], in_=att_tm[:], identity=ident[:]),
                         reads=[R_att, R_c], writes=[R_ps[3]])
                    K.op("act", lambda e, qi=qi: e.copy(out=mixc[:, 128 * qi:128 * (qi + 1)], in_=psb[3][:, 0:128]),
                         reads=[R_ps[3]], writes=[R_mixc])
                if mine:
                    for hf in range(2):
                        K.dma("sp", mixh[hf][c * 128:(c + 1) * 128, :], mixc[:, hf * TH:(hf + 1) * TH],
                              reads=[R_mixc, R_mixg], writes=[R_mixh])
            K.dma("sp", nk_s[l], krow[:], reads=[R_rows], writes=[Res('o')])
            K.dma("sp", nv_s[l], vrow[:], reads=[R_rows], writes=[Res('o')])
            K.barrier()
            ph1.close()
            qbc = sb("qbc", [128, 1024], stack=ph)
            bdmask = sb("bdmask", [16, 1024], stack=ph)
            K.dma("sp", bdmask[:], cst["bdmask"], writes=[R_c])
            prodt = sb("prodt", [128, 1024], stack=ph)
            kblk = [sb("kblk%d" % i, [128, 1024], stack=ph) for i in range(2)]
            vblk = [sb("vblk%d" % i, [128, 1024], stack=ph) for i in range(2)]
            R_kb = [Res("kblk0"), Res("kblk1")]
            R_vb = [Res("vblk0"), Res("vblk1")]
            sc = sb("sc", [128, 17, 16], stack=ph)
            pS = sb("pS", [128, 17, 16], stack=ph)
            numm = sb("numm", [16, 1024], stack=ph)
            dens = sb("dens", [16, 2], stack=ph)
            arow = sb("arow", [1, 1024], stack=ph)
            R_qbc, R_prod, R_sc, R_pS, R_numm, R_dens, R_arow = [Res(n) for n in
                                                                 "qbc prod sc pS numm dens arow".split()]
            for hf in range(2):
                K.op("pe", lambda e, hf=hf: e.matmul(psb[hf][:, :], lhsT=ones_f[0:1, 0:128],
                                                     rhs=qrow[0:1, hf * 512:(hf + 1) * 512], start=True, stop=True),
                     reads=[R_rows, R_c], writes=[R_ps[hf]])
                K.op("act", lambda e, hf=hf: e.copy(out=qbc[:, hf * 512:(hf + 1) * 512], in_=psb[hf][:, :]),
                     reads=[R_ps[hf]], writes=[R_qbc])
            for blk in range(17):
                m = 128 if blk < 16 else 1
                sl = blk % 2
                if blk < 16:
                    K.dma("sp", kblk[sl][:], ck_d[l][blk * 128:(blk + 1) * 128, :], writes=[R_kb[sl]])
                    K.dma("sp", vblk[sl][:], cv_d[l][blk * 128:(blk + 1) * 128, :], writes=[R_vb[sl]])
                    ksrc, vsrc, Rk, Rv = kblk[sl], vblk[sl], R_kb[sl], R_vb[sl]
                else:
                    ksrc, vsrc, Rk, Rv = krow, vrow, R_rows, R_rows
                K.op("dve", lambda e, ksrc=ksrc, m=m: e.tensor_tensor(out=prodt[0:m, :], in0=ksrc[0:m, :], in1=qbc[0:m, :],
                                                                      op=ALU.mult),
                     reads=[Rk, R_qbc], writes=[R_prod])
                K.op("dve", lambda e, blk=blk, m=m: e.tensor_reduce(
                    out=sc[0:m, blk, :], in_=prodt[0:m, :].rearrange("p (h e) -> p h e", e=64), axis=AX.X, op=ALU.add),
                    reads=[R_prod], writes=[R_sc])
                K.op("act", lambda e, blk=blk, m=m: e.activation(out=pS[0:m, blk, :], in_=sc[0:m, blk, :], func=AF.Exp),
                     reads=[R_sc], writes=[R_pS])
                K.op("dve", lambda e, blk=blk, m=m: e.tensor_tensor(out=pS[0:m, blk, :], in0=pS[0:m, blk, :],
                                                                    in1=EsS[0:m, blk, :], op=ALU.mult),
                     reads=[R_pS, R_c], writes=[R_pS])
                for hf in range(2):
                    K.op("pe", lambda e, blk=blk, m=m, hf=hf, vsrc=vsrc: e.matmul(
                        psb[hf][0:16, :], lhsT=pS[0:m, blk, :], rhs=vsrc[0:m, hf * 512:(hf + 1) * 512],
                        start=(blk == 0), stop=(blk == 16)), reads=[R_pS, Rv], writes=[R_ps[hf]])
                K.op("pe", lambda e, blk=blk, m=m: e.matmul(
                    psb[2][0:16, 0:1], lhsT=pS[0:m, blk, :], rhs=ones_f[0:m, 0:1],
                    start=(blk == 0), stop=(blk == 16)), reads=[R_pS, R_c], writes=[R_ps[2]])
            K.op("dve", lambda e: e.reciprocal(out=dens[:, 0:1], in_=psb[2][0:16, 0:1]), reads=[R_ps[2]], writes=[R_dens])
            for hf in range(2):
                K.op("dve", lambda e, hf=hf: e.scalar_tensor_tensor(
                    out=numm[:, hf * 512:(hf + 1) * 512], in0=psb[hf][0:16, :], scalar=dens[:, 0:1],
                    in1=bdmask[:, hf * 512:(hf + 1) * 512], op0=ALU.mult, op1=ALU.mult),
                    reads=[R_ps[hf], R_dens, R_c], writes=[R_numm])
            for hf in range(2):
                K.op("pe", lambda e, hf=hf: e.matmul(psb[3][0:1, :], lhsT=ones_f[0:16, 0:1],
                                                     rhs=numm[:, hf * 512:(hf + 1) * 512], start=True, stop=True),
                     reads=[R_numm, R_c], writes=[R_ps[3]])
                K.op("act", lambda e, hf=hf: e.copy(out=arow[0:1, hf * 512:(hf + 1) * 512], in_=psb[3][0:1, :]),
                     reads=[R_ps[3]], writes=[R_arow])
            for c in range(8):
                K.op("pe", lambda e, c=c: e.matmul(psb[4][:, c:c + 1], lhsT=arow[0:1, c * 128:(c + 1) * 128],
                                                   rhs=ones_f[0:1, 0:1], start=True, stop=True),
                     reads=[R_arow, R_c], writes=[R_ps[4]])
            K.op("act", lambda e: e.copy(out=mixS[:, 0:8], in_=psb[4][:, 0:8]), reads=[R_ps[4]], writes=[R_mixS])
            K.barrier()

    def bcast_rows(dst, j, which, R_dst, np_):
        for g in range(4):
            for jj in range(4):
                kc = 4 * g + jj
                K.op("dve", lambda e, kc=kc: e.tensor_copy(out=bct[:], in_=MOD(j, which, kc).to_broadcast([128, 128])),
                     reads=[R_mod], writes=[R_bct])
                K.op("pe", lambda e, jj=jj: e.matmul(psb[7][:, jj * 128:(jj + 1) * 128], lhsT=bct[:], rhs=ident[:],
                                                     start=True, stop=True),
                     reads=[R_bct, R_c], writes=[R_ps[7]])
            K.op("act", lambda e, g=g: e.copy(out=dst[0:np_, g * 512:(g + 1) * 512], in_=psb[7][0:np_, :]),
                 reads=[R_ps[7]], writes=[R_dst])

    bct = sb("bct", [128, 128])
    R_bct = Res("bct")

    def gmap(g):
        rho, j = g // 8, g % 8
        return 4 * rho + j if j < 4 else 8 + 4 * rho + (j - 4)

    def wout_phase(l):
        for hf in range(2):
            K.op("pool", lambda e, hf=hf: e.collective_compute("AllGather", ALU.bypass, replica_groups=RG,
                                                               ins=[mixh[hf]], outs=[mixg[hf]]),
                 reads=[R_mixh], writes=[R_mixg, R_cc])
        with ExitStack() as ph:
            mixT = sb("mixT", [128, 16, TH], BF16, stack=ph)
            mA = sb("mA", [128, 16, TH], BF16, stack=ph)
            sel = sb("sel", [128, 2], stack=ph)
            gbc_p = sb("gbc_p", [128, D], stack=ph)
            gbc_s = sb("gbc_s", [1, D], stack=ph)
            wo = [sb("wo%d" % i, [128, 16, 512], BF16, stack=ph) for i in range(2)]
            wos = sb("wos", [128, 16, 512], BF16, stack=ph)
            xq = [sb("xq%d" % i, [128, 512], stack=ph) for i in range(2)]
            R_wo = [Res("wo0"), Res("wo1")]
            R_wos = Res("wos")
            R_xq = [Res("xq0"), Res("xq1")]
            R_mT, R_mA, R_g, R_sel = Res("mixTs"), Res("mA"), Res("gbc"), Res("sel")
            K.dma("sp", sel[:], sel_d, writes=[R_sel])
            K.dma("sp", mixT[:], mixg[0].rearrange("(c p) t -> p c t", p=128), reads=[R_mixg], writes=[R_mT])
            K.dma("sp", mA[:], mixg[1].rearrange("(c p) t -> p c t", p=128), reads=[R_mixg], writes=[R_mA])
            TS(mixT[:], mixT[:], sel[:, 0:1], None, ALU.mult, None, [R_mT, R_sel], [R_mT])
            STT(mixT[:], mA[:], sel[:, 1:2], mixT[:], ALU.mult, ALU.add, [R_mA, R_sel, R_mT], [R_mT])
            bcast_rows(gbc_p, 0, "gt1", R_g, 128)
            bcast_rows(gbc_s, 1, "gt1", R_g, 1)
            if l == 0:
                xs_p = lambda i: xph[i * 128:(i + 1) * 128, :]
            else:
                xs_p = lambda i: xhp[i // 2][(i % 2) * 128:(i % 2) * 128 + 128, :]
            xs_s = xs if l == 0 else xsb
            it = 0
            for ng in range(4):
                sl = ng % 2
                K.dma("pool", wo[sl][:], w_view(w_out[l], ng * 512, 512), writes=[R_wo[sl]])
                K.dma("pool", wos[:], w_view(w_out_perm[l], ng * 512, 512), writes=[R_wos])
                for i in list(range(8)) + [16]:
                    np_ = 128 if i < 16 else 1
                    col0 = 128 * i
                    xsrc = xs_p(i)[:, ng * 512:(ng + 1) * 512] if i < 16 else xs_s[:, ng * 512:(ng + 1) * 512]
                    xdst = x1buf[i * 128:(i + 1) * 128, ng * 512:(ng + 1) * 512] if i < 16 else x1buf[TH:TH + 1, ng * 512:(ng + 1) * 512]
                    gb = gbc_p if i < 16 else gbc_s
                    q_ = it % 2
                    it += 1
                    bank = q_
                    K.dma("sp", xq[q_][0:np_, :], xsrc, reads=[R_xh], writes=[R_xq[q_]])
                    for kc in range(16):
                        if i < 16:
                            MM(psb[bank][0:np_, :], mixT[:, kc, col0:col0 + np_], wo[sl][:, gmap(kc), :], kc == 0, kc == 15,
                               [R_mT, R_wo[sl]], [R_ps[bank]])
                        else:
                            MM(psb[bank][0:1, :], mixS[:, kc:kc + 1], wos[:, kc, :], kc == 0, kc == 15,
                               [R_mixS, R_wos], [R_ps[bank]])
                    K.op("dve", lambda e, bank=bank, np_=np_, gb=gb, ng=ng, q_=q_: e.tensor_tensor(
                        out=psb[bank][0:np_, :], in0=psb[bank][0:np_, :], in1=gb[0:np_, ng * 512:(ng + 1) * 512], op=ALU.mult),
                        reads=[R_ps[bank], R_g], writes=[R_ps[bank]])
                    K.op("dve", lambda e, bank=bank, np_=np_, q_=q_: e.tensor_tensor(
                        out=xq[q_][0:np_, :], in0=psb[bank][0:np_, :], in1=xq[q_][0:np_, :], op=ALU.add),
                        reads=[R_ps[bank], R_xq[q_]], writes=[R_xq[q_]])
                    K.dma("sp", xdst, xq[q_][0:np_, :], reads=[R_xq[q_]], writes=[R_x1buf])
            K.barrier()

    FGROUPS = [(1, 512), (513, 512)]

    def ffn_phase(l, last):
        with ExitStack() as ph:
            actT = sb("actT", [128, 44, 513], BF16, stack=ph)
            sgs = sb("sgs", [128, 2], stack=ph)
            R_sgs = Res("sgs")
            gbc_p = sb("gbc2_p", [128, D], stack=ph)
            gbc_s = sb("gbc2_s", [1, D], stack=ph)
            wg = [sb("wg%d" % i, [128, 16, 128], BF16, stack=ph) for i in range(2)]
            wu = [sb("wu%d" % i, [128, 16, 128], BF16, stack=ph) for i in range(2)]
            wd = [sb("wd%d" % i, [128, 11, 512], BF16, stack=ph) for i in range(2)]
            sg = [sb("sg%d" % i, [128, 512], stack=ph) for i in range(2)]
            xq = [sb("xq2_%d" % i, [128, 512], stack=ph) for i in range(5)]
            R_wg = [Res("wg0"), Res("wg1")]
            R_wu = [Res("wu0"), Res("wu1")]
            R_wd = [Res("wd0"), Res("wd1")]
            R_sg = [Res("sg0"), Res("sg1")]
            R_xq = [Res("xq2_%d" % i) for i in range(5)]
            R_aT, R_g = Res("actT"), Res("gbc2")
            bcast_rows(gbc_p, 0, "gt2", R_g, 128)
            bcast_rows(gbc_s, 1, "gt2", R_g, 1)
            wdi = 0
            for gi, (c0, n) in enumerate(FGROUPS):
                for fc in range(44):
                    sl = fc % 2
                    K.dma("pool", wg[sl][:], w_view(w_gu[l], fc * 128, 128), writes=[R_wg[sl]])
                    K.dma("pool", wu[sl][:], w_view(w_gu[l], DFF + fc * 128, 128), writes=[R_wu[sl]])
                    bg, bu = 4 + sl, 6 + sl
                    for kc in range(16):
                        K.op("pe", lambda e, bg=bg, kc=kc, sl=sl, c0=c0, n=n: e.matmul(
                            psb[bg][:, 0:n], lhsT=wg[sl][:, kc, :], rhs=hT[:, kc, c0:c0 + n], start=(kc == 0), stop=(kc == 15)),
                            reads=[R_wg[sl], R_hT], writes=[R_ps[bg]], sig=(kc == 15))
                    for kc in range(16):
                        K.op("pe", lambda e, bu=bu, kc=kc, sl=sl, c0=c0, n=n: e.matmul(
                            psb[bu][:, 0:n], lhsT=wu[sl][:, kc, :], rhs=hT[:, kc, c0:c0 + n], start=(kc == 0), stop=(kc == 15)),
                            reads=[R_wu[sl], R_hT], writes=[R_ps[bu]], sig=(kc == 15))
                    K.op("act", lambda e, bg=bg, sl=sl, n=n: e.activation(out=sg[sl][:, 0:n], in_=psb[bg][:, 0:n], func=AF.Silu),
                         reads=[R_ps[bg]], writes=[R_sg[sl]])
                    K.op("dve", lambda e, bu=bu, sl=sl, n=n, fc=fc: e.tensor_tensor(
                        out=actT[:, fc, 0:n], in0=sg[sl][:, 0:n], in1=psb[bu][:, 0:n], op=ALU.mult),
                        reads=[R_sg[sl], R_ps[bu]], writes=[R_aT])
                    if gi == 1:
                        gb_, ub_ = 2 * sl, 2 * sl + 1
                        for kc in range(16):
                            MM(psb[gb_][:, 0:1], wg[sl][:, kc, :], hT[:, kc, SCOL:SCOL + 1], kc == 0, kc == 15,
                               [R_wg[sl], R_hT], [R_ps[gb_]])
                        for kc in range(16):
                            MM(psb[ub_][:, 0:1], wu[sl][:, kc, :], hT[:, kc, SCOL:SCOL + 1], kc == 0, kc == 15,
                               [R_wu[sl], R_hT], [R_ps[ub_]])
                        ACT(sgs[:, sl:sl + 1], psb[gb_][:, 0:1], AF.Silu, [R_ps[gb_]], [R_sgs])
                        TT(actT[:, fc, 512:513], sgs[:, sl:sl + 1], psb[ub_][:, 0:1], ALU.mult, [R_sgs, R_ps[ub_]], [R_aT])
                ntile = 4 if gi == 0 else 5
                for ng in range(4):
                    for fs in range(4):
                        sl = wdi % 2
                        wdi += 1
                        K.dma("pool", wd[sl][:], w_down[l].rearrange("(fc p) n -> p fc n", p=128)[:, fs * 11:(fs + 1) * 11, ng * 512:(ng + 1) * 512],
                              writes=[R_wd[sl]])
                        for ti in range(ntile):
                            np_ = 128 if ti < 4 else 1
                            for f_ in range(11):
                                fc = fs * 11 + f_
                                K.op("pe", lambda e, ti=ti, fc=fc, f_=f_, sl=sl, np_=np_: e.matmul(
                                    psb[ti][0:np_, :], lhsT=actT[:, fc, ti * 128:ti * 128 + np_], rhs=wd[sl][:, f_, :],
                                    start=(fc == 0), stop=(fc == 43)), reads=[R_aT, R_wd[sl]], writes=[R_ps[ti]],
                                    sig=(f_ == 10))
                    for ti in range(ntile):
                        np_ = 128 if ti < 4 else 1
                        gb = gbc_p if ti < 4 else gbc_s
                        if ti < 4:
                            r0 = gi * 512 + ti * 128
                            xsrc = x1buf[r0:r0 + 128, ng * 512:(ng + 1) * 512]
                            xdst = (y_p[r0:r0 + 128, :] if last else xhp[r0 // 256][r0 % 256:r0 % 256 + 128, :])[:, ng * 512:(ng + 1) * 512]
                        else:
                            xsrc = x1buf[TH:TH + 1, ng * 512:(ng + 1) * 512]
                            xdst = (y_s if last else xsb)[:, ng * 512:(ng + 1) * 512]
                        K.dma("sp", xq[ti][0:np_, :], xsrc, reads=[R_x1buf], writes=[R_xq[ti]])
                        K.op("dve", lambda e, ti=ti, np_=np_, gb=gb, ng=ng: e.tensor_tensor(
                            out=psb[ti][0:np_, :], in0=psb[ti][0:np_, :], in1=gb[0:np_, ng * 512:(ng + 1) * 512], op=ALU.mult),
                            reads=[R_ps[ti], R_g], writes=[R_ps[ti]])
                        K.op("dve", lambda e, ti=ti, np_=np_: e.tensor_tensor(
                            out=xq[ti][0:np_, :], in0=psb[ti][0:np_, :], in1=xq[ti][0:np_, :], op=ALU.add),
                            reads=[R_ps[ti], R_xq[ti]], writes=[R_xq[ti]])
                        K.dma("sp", xdst, xq[ti][0:np_, :], reads=[R_xq[ti]], writes=[Res('o') if last else R_xh])
            K.barrier()

    for l in range(nl):
        adaln_phase(l)
        if l == 0:
            xsrc_p = lambda i: xp[i * 128:(i + 1) * 128, :]
        else:
            xsrc_p = lambda i: xg[(i % 8) // 2][(i // 8) * 256 + (i % 2) * 128:(i // 8) * 256 + (i % 2) * 128 + 128, :]
        xsrc_s = xs if l == 0 else xsb
        norm_phase(l, xsrc_p, xsrc_s, A1, "sh1", True)
        if stop_after == "norm1":
            break
        att_phase(l)
        if stop_after == "att":
            break
        if "rwkv" not in skip:
            rwkv_phase(l)
        if stop_after == "rwkv":
            break
        wout_phase(l)
        if stop_after == "wout":
            break
        norm_phase(l, (lambda i: x1buf[i * 128:(i + 1) * 128, :]), x1buf[TH:TH + 1, :], A2, "sh2", False, ntp=8)
        if stop_after == "norm2":
            break
        ffn_phase(l, l == NL - 1)
        if l < NL - 1:
            for k in range(4):
                K.op("pool", lambda e, k=k: e.collective_compute("AllGather", ALU.bypass, replica_groups=RG,
                                                                ins=[xhp[k]], outs=[xg[k]]),
                     reads=[R_xh], writes=[R_xfull, R_cc])
            K.barrier()

    K.barrier()
    block = es.enter_context(nc.Block())

    @block.tensor
    def _(e):
        K.emit(e, "pe")

    @block.scalar
    def _(e):
        K.emit(e, "act")

    @block.vector
    def _(e):
        K.emit(e, "dve")

    @block.gpsimd
    def _(e):
        K.emit(e, "pool")

    @block.sync
    def _(e):
        K.emit(e, "sp")

    es.close()
    return nc


def perms(r):
    cp = [4 * r + j for j in range(4)] + [4 * (1 - r) + j for j in range(4)]
    featp = np.concatenate([np.arange(128) + 128 * c for c in cp])
    headp = np.array([2 * c + h for c in cp for h in range(2)])
    return cp, featp, headp


def rank_shared(inp, r):
    f = lambda a: np.ascontiguousarray(np.asarray(a, dtype=np.float32))
    cp, featp, headp = perms(r)
    m = {}
    cols = np.concatenate([g * 1024 + featp for g in range(6)])
    m["w_in"] = f(inp["w_in"][:, :, cols])
    rows = np.concatenate([featp, 1024 + featp])
    m["w_out_perm"] = f(inp["w_out"][:, rows, :])
    m["decay_w2"] = f(inp["decay_w2"][:, :, featp])
    m["aaa_w2"] = f(inp["aaa_w2"][:, :, featp])
    m["gate_w2"] = f(inp["gate_w2"][:, :, featp])
    m["vres_w2"] = f(inp["vres_w2"][:, :, featp])
    m["relb"] = f(inp["rel_bias"][:, headp])
    vecs = np.zeros((128, NL, NV), np.float32)

    def put(l, name, arr):
        o, w = VOFF[name]
        vecs[:, l, o:o + w] = arr
    for l in range(NL):
        put(l, "ada_b", fm(inp["ada_b"][l], 96))
        put(l, "g1", fm(inp["norm1_g"][l], 16))
        put(l, "g2", fm(inp["norm2_g"][l], 16))
        put(l, "mu_w", fm(inp["mu_wag"][l, 0], 16))
        put(l, "mu_a", fm(inp["mu_wag"][l, 1], 16))
        put(l, "mu_g", fm(inp["mu_wag"][l, 2], 16))
        put(l, "mu_r", fm(inp["mu_rkv"][l, 0], 8)[:, cp])
        put(l, "mu_k", fm(inp["mu_rkv"][l, 1], 8)[:, cp])
        put(l, "mu_v", fm(inp["mu_rkv"][l, 2], 8)[:, cp])
        put(l, "w0", fm(inp["decay_w0"][l], 8)[:, cp])
        put(l, "a0", fm(inp["aaa_a0"][l], 8)[:, cp])
        if l > 0:
            put(l, "vmu", fm(inp["vres_mu"][l - 1], 16))
            put(l, "v0", fm(inp["vres_v0"][l - 1], 8)[:, cp])
        put(l, "k_k", fm(inp["k_k"][l], 8)[:, cp])
        put(l, "k_a", fm(inp["k_a"][l], 8)[:, cp])
        put(l, "r_k", fm(np.asarray(inp["r_k"][l]).reshape(-1), 8)[:, cp])
        put(l, "lnx_g", fm(inp["lnx_g"][l], 8)[:, cp])
        put(l, "lnx_b", fm(inp["lnx_b"][l], 8)[:, cp])
        put(l, "gq", np.tile(np.asarray(inp["q_norm_g"][l], np.float32), 2)[:, None])
        put(l, "gk", np.tile(np.asarray(inp["k_norm_g"][l], np.float32), 2)[:, None])
    m["vecs"] = vecs
    sel = np.zeros((128, 2), np.float32)
    sel[:, r] = 1.0
    m["sel"] = sel
    for n in ("ada_w", "w_out", "w_gu", "w_down", "decay_w1", "aaa_w1", "gate_w1", "vres_w1"):
        m[n] = f(inp[n])
    return m


def make_in_map(inp, core, consts, shared):
    b = core % 4
    r = core // 4
    i = core
    f = lambda a: np.ascontiguousarray(np.asarray(a, dtype=np.float32))
    cp, featp, headp = perms(r)
    m = dict(shared[r])
    m["xp"] = f(inp["x_prompt"][b])
    m["xph"] = f(inp["x_prompt"][b, TH * r:TH * (r + 1)])
    m["xs"] = f(inp["x_sample"][i])
    cT = np.stack([fm(inp["c_prompt"][b], 16), fm(inp["c_sample"][i], 16)], axis=-1)
    m["cT"] = f(cT)
    m["ck"] = f(np.asarray(inp["cache_k"])[:, i][:, :, headp, :].reshape(NL, 2048, 1024))
    m["cv"] = f(np.asarray(inp["cache_v"])[:, i][:, :, headp, :].reshape(NL, 2048, 1024))
    m["swkv"] = f(np.asarray(inp["state_wkv"])[:, i][:, headp])
    m["sshT"] = f(np.stack([fm(inp["state_shift"][l, i], 16) for l in range(NL)]))
    for n, a in consts.items():
        m["c_" + n] = a
    return m


def unfm(a):
    return np.ascontiguousarray(a.T).reshape(-1)


def assemble(R):
    y_p = np.zeros((4, T, D), np.float32)
    nk_p = np.zeros((NL, 4, T, 16, 64), np.float32)
    nv_p = np.zeros((NL, 4, T, 16, 64), np.float32)
    nwkv_p = np.zeros((NL, 4, 16, 64, 64), np.float32)
    nk_s = np.zeros((NL, 8, 1, 16, 64), np.float32)
    nv_s = np.zeros((NL, 8, 1, 16, 64), np.float32)
    nwkv_s = np.zeros((NL, 8, 16, 64, 64), np.float32)
    for core in range(8):
        if R[core] is None:
            continue
        b, r = core % 4, core // 4
        cp, featp, headp = perms(r)
        y_p[b, TH * r:TH * (r + 1)] = R[core]["y_p"]
        nk_p[:, b, :, 8 * r:8 * r + 8, :] = np.asarray(R[core]["nk_p"]).reshape(NL, T, 8, 64)
        nv_p[:, b, :, 8 * r:8 * r + 8, :] = np.asarray(R[core]["nv_p"]).reshape(NL, T, 8, 64)
        nwkv_p[:, b, 8 * r:8 * r + 8] = R[core]["nwkv_p"]
        nk_s[:, core, :, headp, :] = np.moveaxis(np.asarray(R[core]["nk_s"]).reshape(NL, 1, 16, 64), 2, 0)
        nv_s[:, core, :, headp, :] = np.moveaxis(np.asarray(R[core]["nv_s"]).reshape(NL, 1, 16, 64), 2, 0)
        nwkv_s[:, core, headp] = np.asarray(R[core]["nwkv_s"])
    y_s = np.stack([R[i]["y_s"] for i in range(8)]).astype(np.float32)
    nsh_p = np.stack([np.stack([unfm(R[b]["nsh_p"][l]) for b in range(4)]) for l in range(NL)])
    nsh_s = np.stack([np.stack([unfm(R[i]["nsh_s"][l]) for i in range(8)]) for l in range(NL)])
    outs = (y_p, y_s, nk_p, nv_p, nwkv_p, nsh_p, nk_s, nv_s, nwkv_s, nsh_s)
    return tuple(np.ascontiguousarray(o, dtype=np.float32) for o in outs)


def kernel(**inputs):
    consts = host_consts()
    inp = {k: np.asarray(v) for k, v in inputs.items()}
    nc = build()
    shared = [rank_shared(inp, r) for r in range(2)]
    in_maps = [make_in_map(inp, c, consts, shared) for c in range(8)]
    res = run_bass_kernel_spmd(nc, in_maps, core_ids=list(range(8)))
    return assemble(res.results)
```

```python
import math
import numpy as np
from contextlib import ExitStack
import concourse.bass as bass
import concourse.mybir as mybir
from concourse.bass_utils import run_bass_kernel_spmd

F32 = mybir.dt.float32
F32R = mybir.dt.float32r
BF16 = mybir.dt.bfloat16
AF = mybir.ActivationFunctionType
ALU = mybir.AluOpType
AX = mybir.AxisListType

D = 2048
T = 2048
NL = 4
TH = 1024
RG = [[0, 4], [1, 5], [2, 6], [3, 7]]
NCOL = 2051
SCOL = 2050
NTILES = [(0, 512), (512, 512), (1024, 512), (1536, 512), (2048, 3)]
DFF = 5632
EPS = 1e-6
GN_EPS = 64e-5
EW = 2176

VOFF = {}
_o = 0
for _n, _w in [("ada_b", 96), ("g1", 16), ("g2", 16), ("mu_w", 16), ("mu_a", 16), ("mu_g", 16),
               ("mu_r", 8), ("mu_k", 8), ("mu_v", 8), ("w0", 8), ("a0", 8), ("vmu", 16), ("v0", 8),
               ("k_k", 8), ("k_a", 8), ("r_k", 8), ("lnx_g", 8), ("lnx_b", 8), ("gq", 1), ("gk", 1)]:
    VOFF[_n] = (_o, _w)
    _o += _w
NV = _o


class Res:
    __slots__ = ("name", "w", "r", "excl")

    def __init__(self, name, excl=False):
        self.name = name
        self.w = {}
        self.r = {}
        self.excl = excl


class Sched:
    ENG = ("pe", "act", "dve", "pool", "sp")
    NDS = 8

    def __init__(self, nc, es):
        self.nc = nc
        self.q = {e: [] for e in self.ENG}
        self.sem = {e: es.enter_context(nc.semaphore("S_" + e)) for e in self.ENG}
        self.cnt = {e: 0 for e in self.ENG}
        self.seen = {e: {} for e in self.ENG}
        self.pend = {e: ([], []) for e in self.ENG}
        self.dsem = {e: [es.enter_context(nc.semaphore("D_%s%d" % (e, i))) for i in range(self.NDS)]
                     for e in ("sp", "pool")}
        self.dval = {e: [0] * self.NDS for e in ("sp", "pool")}
        self.dnext = {e: 0 for e in ("sp", "pool")}
        self.semobj = {}
        for e in self.ENG:
            self.semobj["S_" + e] = self.sem[e]
        for e in ("sp", "pool"):
            for i in range(self.NDS):
                self.semobj["D_%s%d" % (e, i)] = self.dsem[e][i]
        self.ninstr = 0

    def _deps(self, eng, reads, writes):
        need = {}
        for r in reads:
            for k, v in r.w.items():
                if need.get(k, 0) < v:
                    need[k] = v
        for w in writes:
            for k, v in w.w.items():
                if need.get(k, 0) < v:
                    need[k] = v
            for k, v in w.r.items():
                if need.get(k, 0) < v:
                    need[k] = v
        waits = []
        seen = self.seen[eng]
        own = "S_" + eng
        for k, v in need.items():
            if k == own and eng == "pe":
                continue
            if seen.get(k, 0) >= v:
                continue
            seen[k] = v
            waits.append((self.semobj[k], v))
        return waits

    def op(self, eng, fn, reads=(), writes=(), sig=True):
        ex = [r for r in reads if r.excl]
        if ex:
            writes = list(writes) + ex
        waits = self._deps(eng, reads, writes)
        pr, pw = self.pend[eng]
        pr.extend(reads)
        pw.extend(writes)
        inc = None
        if sig:
            self.cnt[eng] += 1
            key = "S_" + eng
            val = self.cnt[eng]
            for r in pr:
                r.r[key] = val
            for w in pw:
                w.w = {key: val}
                w.r = {}
            self.pend[eng] = ([], [])
            inc = (self.sem[eng], 1)
        self.q[eng].append((waits, fn, inc))
        self.ninstr += 1

    def dma(self, qn, out, in_, reads=(), writes=(), **kw):
        k = self.dnext[qn]
        self.dnext[qn] = (k + 1) % (2 if qn == "pool" else self.NDS)
        key = "D_%s%d" % (qn, k)
        waits = self._deps(qn, reads, writes)
        prev = self.dval[qn][k]
        if prev > 0 and self.seen[qn].get(key, 0) < prev:
            self.seen[qn][key] = prev
            waits.append((self.semobj[key], prev))
        val = prev + 16
        self.dval[qn][k] = val
        for r in reads:
            r.r[key] = val
        for w in writes:
            w.w = {key: val}
            w.r = {}
        self.q[qn].append((waits, (lambda e: e.dma_start(out=out, in_=in_, **kw)), (self.semobj[key], 16)))
        self.ninstr += 1

    def barrier(self):
        cur = {}
        for e in self.ENG:
            if self.cnt[e] > 0:
                cur["S_" + e] = self.cnt[e]
        for e in ("sp", "pool"):
            for i in range(self.NDS):
                if self.dval[e][i] > 0:
                    cur["D_%s%d" % (e, i)] = self.dval[e][i]
        for e in self.ENG:
            waits = []
            for k, v in cur.items():
                if k == "S_" + e and e == "pe":
                    continue
                if self.seen[e].get(k, 0) < v:
                    self.seen[e][k] = v
                    waits.append((self.semobj[k], v))
            if waits:
                self.q[e].append((waits, None, None))

    def emit(self, e, name):
        for waits, fn, inc in self.q[name]:
            for sem, val in waits:
                e.wait_ge(sem, val)
            if fn is None:
                continue
            ins = fn(e)
            if inc is not None:
                ins.then_inc(inc[0], inc[1])


def bucket_np(d):
    d = np.asarray(d)
    dd = np.maximum(d, 0)
    df = np.maximum(dd, 1).astype(np.float32)
    large = 16 + (np.log(df / np.float32(16)) / np.float32(math.log(2048 / 16)) * np.float32(16)).astype(np.int32)
    return np.where(dd < 16, dd, np.minimum(large, 31))


def count_np(d):
    d = np.asarray(d)
    c = (d <= 128).astype(np.float32) + ((d % 4 == 0) & (d <= 512)) + ((d % 16 == 0) & (d <= 2048))
    return np.where(d >= 0, c, 0).astype(np.float32)


def host_consts():
    c = {}
    c["ident"] = np.eye(128, dtype=np.float32)
    bo = np.zeros((128, 128), np.float32)
    bo[:64, :64] = 1
    bo[64:, 64:] = 1
    c["blockones"] = bo
    s = np.arange(128)[:, None]
    t = np.arange(128)[None, :]
    c["m_lt"] = (s < t).astype(np.float32)
    c["m_le"] = (s <= t).astype(np.float32)
    c["m_gt"] = (s > t).astype(np.float32)
    dd = np.arange(EW) - 127
    bk = bucket_np(np.maximum(dd, 0))
    oh = np.zeros((32, EW), np.float32)
    oh[bk, np.arange(EW)] = 1
    c["oh"] = oh
    c["cnt16"] = np.broadcast_to(count_np(dd)[None, :], (16, EW)).astype(np.float32).copy()
    ds = np.concatenate([2048 - np.arange(2048), [0]])
    ohs = np.zeros((32, 2049), np.float32)
    ohs[bucket_np(ds), np.arange(2049)] = 1
    c["ohs"] = ohs
    cs = count_np(ds)
    cnts = np.zeros((128, 17), np.float32)
    cnts[:, :16] = cs[:2048].reshape(16, 128).T
    cnts[0, 16] = cs[2048]
    c["cnts"] = cnts
    bd = np.zeros((16, 1024), np.float32)
    for h in range(16):
        bd[h, h * 64:(h + 1) * 64] = 1
    c["bdmask"] = bd
    return c


def fm(v, nch):
    return np.ascontiguousarray(np.asarray(v, np.float32).reshape(nch, 128).T)


def build(nl=NL, stop_after=None, dbg=False, skip=(), rw_c=8, rw_nch=16, rw_sample=True):
    nc = bass.Bass("TRN2", target_bir_lowering=False)
    es = ExitStack()
    K = Sched(nc, es)

    def din(name, shape, dt=F32):
        return nc.dram_tensor(name, list(shape), dt, kind="ExternalInput").ap()

    def dout(name, shape, dt=F32):
        return nc.dram_tensor(name, list(shape), dt, kind="ExternalOutput").ap()

    def dscr(name, shape, dt=F32):
        return nc.dram_tensor(name, list(shape), dt).ap()

    uniq = [0]

    def sb(name, shape, dt=F32, stack=es):
        uniq[0] += 1
        return stack.enter_context(nc.sbuf_tensor("s%d_%s" % (uniq[0], name), list(shape), dt))

    xp = din("xp", [T, D])
    xph = din("xph", [TH, D])
    sel_d = din("sel", [128, 2])
    w_out_perm = din("w_out_perm", [NL, D, D])
    xs = din("xs", [1, D])
    cT_d = din("cT", [128, 16, 2])
    ck_d = din("ck", [NL, 2048, 1024])
    cv_d = din("cv", [NL, 2048, 1024])
    swkv_d = din("swkv", [NL, 16, 64, 64])
    sshT_d = din("sshT", [NL, 128, 16])
    relb_d = din("relb", [32, 16])
    vecs_d = din("vecs", [128, NL, NV])
    ada_w = din("ada_w", [NL, D, 6 * D])
    w_in = din("w_in", [NL, D, 6144])
    w_out = din("w_out", [NL, D, D])
    w_gu = din("w_gu", [NL, D, 2 * DFF])
    w_down = din("w_down", [NL, DFF, D])
    dw1 = din("decay_w1", [NL, D, 64])
    dw2 = din("decay_w2", [NL, 64, 1024])
    aw1 = din("aaa_w1", [NL, D, 64])
    aw2 = din("aaa_w2", [NL, 64, 1024])
    gw1 = din("gate_w1", [NL, D, 160])
    gw2 = din("gate_w2", [NL, 160, 1024])
    vw1 = din("vres_w1", [3, D, 32])
    vw2 = din("vres_w2", [3, 32, 1024])
    cst = {}
    for n, shp in [("ident", [128, 128]), ("blockones", [128, 128]), ("m_lt", [128, 128]), ("m_le", [128, 128]),
                   ("m_gt", [128, 128]), ("oh", [32, EW]), ("cnt16", [16, EW]), ("ohs", [32, 2049]),
                   ("cnts", [128, 17]), ("bdmask", [16, 1024])]:
        cst[n] = din("c_" + n, shp)

    y_p = dout("y_p", [TH, D])
    y_s = dout("y_s", [1, D])
    nk_p = dout("nk_p", [NL, T, 512])
    nv_p = dout("nv_p", [NL, T, 512])
    nwkv_p = dout("nwkv_p", [NL, 8, 64, 64])
    nsh_p = dout("nsh_p", [NL, 128, 16])
    nk_s = dout("nk_s", [NL, 1, 1024])
    nv_s = dout("nv_s", [NL, 1, 1024])
    nwkv_s = dout("nwkv_s", [NL, 16, 64, 64])
    nsh_s = dout("nsh_s", [NL, 128, 16])

    xg = [dscr("xg%d" % k, [512, D]) for k in range(4)]
    xhp = [dscr("xhp%d" % k, [256, D]) for k in range(4)]
    R_cc = Res("cc")
    xsb = dscr("xsb", [1, D])
    x1buf = dscr("x1buf", [TH + 1, D])
    estrip = dscr("estrip", [16, 128, 2048])
    mixh = [dscr("mixh%d" % i, [8 * 128, TH], BF16) for i in range(2)]
    mixg = [dscr("mixg%d" % i, [16 * 128, TH], BF16) for i in range(2)]
    R_mixh, R_mixg, R_xfull, R_xh = Res("mixh"), Res("mixg"), Res("xfull"), Res("xh")
    vfirst_d = dscr("vfirst_d", [8, 128, NCOL])
    R_xbuf, R_x1buf, R_estrip, R_vfirst = Res("xbuf"), Res("x1buf"), Res("estrip"), Res("vfirst")
    R_out = Res("outs")

    ident = sb("ident", [128, 128])
    blockones = sb("blockones", [128, 128])
    m_lt = sb("m_lt", [128, 128])
    m_le = sb("m_le", [128, 128])
    m_gt = sb("m_gt", [128, 128])
    vecs = sb("vecs", [128, NL, NV])
    EsS = sb("EsS", [128, 17, 16])
    ones_f = sb("ones_f", [128, 128])
    sc_bf = sb("sc_bf", [128, 16, 2], BF16)
    hT = sb("hT", [128, 16, NCOL], BF16)
    R_c = Res("consts")
    R_hT = Res("hT")
    psb = [es.enter_context(nc.psum_tensor("ps%d" % i, [128, 512], F32)) for i in range(8)]
    R_ps = [Res("ps%d" % i, excl=True) for i in range(8)]

    for t_, n in [(ident, "ident"), (blockones, "blockones"), (m_lt, "m_lt"), (m_le, "m_le"), (m_gt, "m_gt")]:
        K.dma("sp", t_[:], cst[n], writes=[R_c])
    K.dma("sp", vecs[:], vecs_d, writes=[R_c])
    K.op("dve", lambda e: e.memset(ones_f[:], 1.0), writes=[R_c])
    K.op("dve", lambda e: e.memset(hT[:, :, 0:1], 0.0), writes=[R_hT])

    def V(l, name, j=None):
        o, w = VOFF[name]
        if j is None:
            return vecs[:, l, o:o + w]
        return vecs[:, l, o + j:o + j + 1]

    with ExitStack() as p0:
        oh = sb("oh", [32, EW], stack=p0)
        cnt16 = sb("cnt16", [16, EW], stack=p0)
        ohs = sb("ohs", [32, 2049], stack=p0)
        cnts = sb("cnts", [128, 17], stack=p0)
        relb = sb("relb", [32, 16], stack=p0)
        Esb = sb("Esb", [16, EW], stack=p0)
        cT = sb("cTs", [128, 16, 2], stack=p0)
        R0 = Res("p0")
        R_E = Res("Esb")
        for t_, src in [(oh, cst["oh"]), (cnt16, cst["cnt16"]), (ohs, cst["ohs"]), (cnts, cst["cnts"]),
                        (relb, relb_d), (cT, cT_d)]:
            K.dma("sp", t_[:], src, writes=[R0])
        K.op("act", lambda e: e.activation(out=sc_bf[:], in_=cT[:], func=AF.Silu), reads=[R0], writes=[R_c])
        c0 = 0
        bi = 0
        while c0 < EW:
            n = min(512, EW - c0)
            ps = psb[bi % 2]
            K.op("pe", lambda e, ps=ps, c0=c0, n=n: e.matmul(ps[0:16, 0:n], lhsT=relb[:, :], rhs=oh[:, c0:c0 + n],
                                                            start=True, stop=True),
                 reads=[R0], writes=[R_ps[bi % 2]])
            K.op("act", lambda e, ps=ps, c0=c0, n=n: e.activation(out=Esb[:, c0:c0 + n], in_=ps[0:16, 0:n], func=AF.Exp),
                 reads=[R_ps[bi % 2]], writes=[R_E])
            c0 += n
            bi += 1
        K.op("dve", lambda e: e.tensor_tensor(out=Esb[:], in0=Esb[:], in1=cnt16[:], op=ALU.mult),
             reads=[R0, R_E], writes=[R_E])
        for p in range(128):
            K.dma("sp", estrip[:, p, :], Esb[:, 127 - p:127 - p + 2048], reads=[R_E], writes=[R_estrip])
        for blk in range(17):
            m = 128 if blk < 16 else 1
            ps = psb[2 + blk % 2]
            K.op("pe", lambda e, ps=ps, blk=blk, m=m: e.matmul(ps[0:m, 0:16], lhsT=ohs[:, blk * 128:blk * 128 + m],
                                                              rhs=relb[:, :], start=True, stop=True),
                 reads=[R0], writes=[R_ps[2 + blk % 2]])
            K.op("act", lambda e, ps=ps, blk=blk, m=m: e.activation(out=EsS[0:m, blk, :], in_=ps[0:m, 0:16], func=AF.Exp,
                                                                   scale=1.0),
                 reads=[R_ps[2 + blk % 2]], writes=[R_c])
            K.op("dve", lambda e, blk=blk, m=m: e.tensor_scalar(out=EsS[0:m, blk, :], in0=EsS[0:m, blk, :],
                                                                scalar1=cnts[0:m, blk:blk + 1], scalar2=None,
                                                                op0=ALU.mult),
                 reads=[R0, R_c], writes=[R_c])
        K.barrier()


    SQD = math.sqrt(D)
    modt = sb("modt", [128, 2, 96])
    A1 = sb("A1", [128, 2, 16])
    A2 = sb("A2", [128, 2, 16])
    hl = sb("hl", [128, 2, 16])
    R_mod = Res("mod")
    R_hl = Res("hl")

    def MOD(j, which, kc=None):
        o = {"sh1": 0, "sc1": 16, "gt1": 32, "sh2": 48, "sc2": 64, "gt2": 80}[which]
        if kc is None:
            return modt[:, j, o:o + 16]
        return modt[:, j, o + kc:o + kc + 1]

    def w_view(w_l, c0, n):
        return w_l.rearrange("(kc p) n -> p kc n", p=128)[:, :, c0:c0 + n]

    def rsqrt(out, in_, addc, reads, writes):
        K.op("act", lambda e: e.activation(out=out, in_=in_, func=AF.Ln, bias=float(addc), scale=1.0),
             reads=reads, writes=writes)
        K.op("act", lambda e: e.activation(out=out, in_=out, func=AF.Exp, scale=-0.5), reads=writes, writes=writes)

    def norm_phase(l, xsrc_p, xsrc_s, Aa, shname, out_hl, ntp=16):
        with ExitStack() as ph:
            xt = [sb("xt%d" % i, [128, D], stack=ph) for i in range(2)]
            R_xt = [Res("xt0"), Res("xt1")]
            junk = sb("junk", [128, D], BF16, stack=ph)
            ssq = sb("ssq", [128, 2], stack=ph)
            R_junk, R_ssq = Res("junk"), Res("ssq")
            for i in list(range(ntp)) + [16]:
                np_ = 128 if i < 16 else 1
                j = 0 if i < 16 else 1
                col0 = 1 + 128 * i if i < 16 else SCOL
                src = xsrc_p(i) if i < 16 else xsrc_s
                s_ = i % 2
                xt_, Rx = xt[s_], R_xt[s_]
                K.dma("sp", xt_[0:np_, :], src, reads=[R_xbuf, R_x1buf, R_xfull, R_xh], writes=[Rx])
                K.op("act", lambda e, xt_=xt_, np_=np_, s_=s_: e.activation(
                    out=junk[0:np_, :], in_=xt_[0:np_, :], func=AF.Square, accum_out=ssq[0:np_, s_:s_ + 1]),
                    reads=[Rx], writes=[R_junk, R_ssq])
                rsqrt(ssq[0:np_, s_:s_ + 1], ssq[0:np_, s_:s_ + 1], D * EPS, [R_ssq], [R_ssq])
                K.op("dve", lambda e, xt_=xt_, np_=np_, s_=s_: e.tensor_scalar(
                    out=xt_[0:np_, :], in0=xt_[0:np_, :], scalar1=ssq[0:np_, s_:s_ + 1], scalar2=None,
                    op0=ALU.mult), reads=[R_ssq, Rx], writes=[Rx])
                for b in range(4):
                    bank = 4 * s_ + b
                    for jj in range(4):
                        kc = 4 * b + jj
                        K.op("pe", lambda e, bank=bank, jj=jj, kc=kc, xt_=xt_, np_=np_: e.transpose(
                            out=psb[bank][:, jj * 128:jj * 128 + np_], in_=xt_[0:np_, kc * 128:(kc + 1) * 128],
                            identity=ident[0:np_, 0:np_]), reads=[Rx, R_c], writes=[R_ps[bank]], sig=(jj == 3))
                    for jj in range(4):
                        kc = 4 * b + jj
                        K.op("act", lambda e, bank=bank, jj=jj, kc=kc, np_=np_, col0=col0, j=j: e.activation(
                            out=hT[:, kc, col0:col0 + np_], in_=psb[bank][:, jj * 128:jj * 128 + np_],
                            func=AF.Identity, scale=Aa[:, j, kc:kc + 1], bias=MOD(j, shname, kc)),
                            reads=[R_ps[bank], R_mod], writes=[R_hT], sig=(jj == 3 and (not out_hl or i < 15)))
                    if out_hl and i >= 15:
                        for jj in range(4):
                            kc = 4 * b + jj
                            K.op("act", lambda e, bank=bank, jj=jj, kc=kc, np_=np_, j=j: e.activation(
                                out=hl[:, j, kc:kc + 1], in_=psb[bank][:, jj * 128 + np_ - 1:jj * 128 + np_],
                                func=AF.Identity, scale=Aa[:, j, kc:kc + 1], bias=MOD(j, shname, kc)),
                                reads=[R_ps[bank], R_mod], writes=[R_hl], sig=(jj == 3))
            if out_hl:
                K.dma("sp", nsh_p[l], hl[:, 0, :], reads=[R_hl], writes=[R_out])
                K.dma("sp", nsh_s[l], hl[:, 1, :], reads=[R_hl], writes=[R_out])
            K.barrier()

    def adaln_phase(l):
        with ExitStack() as ph:
            wb = [sb("wb%d" % i, [128, 16, 512], BF16, stack=ph) for i in range(2)]
            R_wb = [Res("wb0"), Res("wb1")]
            tmp = sb("tmpa", [128, 16], stack=ph)
            shs = sb("shs", [128, 16], stack=ph)
            R_t = Res("tmpa")
            psM = psb[7]
            for og in range(24):
                s_ = og % 2
                K.dma("pool", wb[s_][:], w_view(ada_w[l], og * 512, 512), writes=[R_wb[s_]])
                for oc in range(4):
                    cc = og * 4 + oc
                    for kc in range(16):
                        K.op("pe", lambda e, s_=s_, oc=oc, cc=cc, kc=kc: e.matmul(
                            psM[:, 2 * cc:2 * cc + 2], lhsT=wb[s_][:, kc, oc * 128:(oc + 1) * 128],
                            rhs=sc_bf[:, kc, :], start=(kc == 0), stop=(kc == 15)),
                            reads=[R_wb[s_], R_c], writes=[R_ps[7]], sig=(kc == 15))
            for j in range(2):
                K.op("dve", lambda e, j=j: e.tensor_tensor(
                    out=modt[:, j, :], in0=psM[:, j:192:2], in1=V(l, "ada_b"), op=ALU.add),
                    reads=[R_ps[7], R_c], writes=[R_mod])
            for j in range(2):
                for Aa, scn, gn in ((A1, "sc1", "g1"), (A2, "sc2", "g2")):
                    K.op("dve", lambda e, j=j, scn=scn: e.tensor_scalar(
                        out=tmp[:], in0=MOD(j, scn), scalar1=1.0, scalar2=SQD, op0=ALU.add, op1=ALU.mult),
                        reads=[R_mod], writes=[R_t])
                    K.op("dve", lambda e, j=j, Aa=Aa, gn=gn: e.tensor_tensor(
                        out=Aa[:, j, :], in0=tmp[:], in1=V(l, gn), op=ALU.mult),
                        reads=[R_t, R_c], writes=[R_mod])
            K.dma("sp", shs[:], sshT_d[l], writes=[R_t])
            K.op("dve", lambda e: e.tensor_copy(out=hT[:, :, 2049], in_=shs[:]), reads=[R_t], writes=[R_hT])
            K.barrier()

    mixS = sb("mixS", [128, 16], BF16)
    R_mixS = Res("mixS")
    gk8 = sb("gk8", [128, NL])
    for l_ in range(NL):
        K.op("dve", lambda e, l_=l_: e.tensor_scalar(out=gk8[:, l_:l_ + 1], in0=V(l_, "gk"), scalar1=8.0,
                                                    scalar2=None, op0=ALU.mult), reads=[R_c], writes=[R_c])

    def proj_chunk(l, wslot, R_w, colbase, dstf, R_dst, pbanks, wsrc=None, kch=16, src_off=0, tiles=None):
        K.dma("pool", wslot[:], w_view(w_in[l] if wsrc is None else wsrc, colbase, 128), reads=[], writes=[R_w])
        for nt, (c0, n) in enumerate(NTILES if tiles is None else tiles):
            bank = pbanks[nt % 2]
            for kc in range(kch):
                K.op("pe", lambda e, bank=bank, kc=kc, c0=c0, n=n: e.matmul(
                    psb[bank][:, 0:n], lhsT=wslot[:, kc, :], rhs=hT[:, kc, c0:c0 + n],
                    start=(kc == 0), stop=(kc == kch - 1)),
                    reads=[R_w, R_hT], writes=[R_ps[bank]], sig=(kc == kch - 1))
            K.op("act", lambda e, bank=bank, c0=c0, n=n: e.copy(out=dstf[:, c0:c0 + n], in_=psb[bank][:, 0:n]),
                 reads=[R_ps[bank]], writes=[R_dst])

    def TT(out, in0, in1, op, reads, writes, eng="dve"):
        K.op(eng, lambda e: e.tensor_tensor(out=out, in0=in0, in1=in1, op=op), reads=reads, writes=writes)

    def TS(out, in0, s1, s2, op0, op1, reads, writes):
        if s2 is None:
            K.op("dve", lambda e: e.tensor_scalar(out=out, in0=in0, scalar1=s1, scalar2=None, op0=op0), reads=reads, writes=writes)
        else:
            K.op("dve", lambda e: e.tensor_scalar(out=out, in0=in0, scalar1=s1, scalar2=s2, op0=op0, op1=op1), reads=reads, writes=writes)

    def STT(out, in0, scalar, in1, op0, op1, reads, writes):
        K.op("dve", lambda e: e.scalar_tensor_tensor(out=out, in0=in0, scalar=scalar, in1=in1, op0=op0, op1=op1),
             reads=reads, writes=writes)

    def ACT(out, in_, func, reads, writes, bias=None, scale=1.0):
        if bias is None:
            K.op("act", lambda e: e.activation(out=out, in_=in_, func=func, scale=scale), reads=reads, writes=writes)
        else:
            K.op("act", lambda e: e.activation(out=out, in_=in_, func=func, bias=bias, scale=scale), reads=reads, writes=writes)

    def MM(out, lhsT, rhs, start, stop, reads, writes, sig=None):
        K.op("pe", lambda e: e.matmul(out, lhsT=lhsT, rhs=rhs, start=start, stop=stop), reads=reads, writes=writes,
             sig=(stop if sig is None else sig))

    def TR(out, in_, idn, reads, writes):
        K.op("pe", lambda e: e.transpose(out=out, in_=in_, identity=idn), reads=reads, writes=writes)

    LGROUPS = [(1, 512), (513, 512), (1025, 512), (1537, 512), (SCOL, 1)]
    LNW = -math.exp(-0.5)

    def rwkv_phase(l):
        with ExitStack() as ph:
            lows_da = sb("lows_da", [128, NCOL], BF16, stack=ph)
            lows_g = sb("lows_g", [128, NCOL], BF16, stack=ph)
            lows_gv = sb("lows_gv", [64, NCOL], BF16, stack=ph)
            W2da = sb("W2da", [128, 1024], BF16, stack=ph)
            W2g = sb("W2g", [128, 1024], BF16, stack=ph)
            W2gv = sb("W2gv", [64, 1024], BF16, stack=ph)
            R_lows, R_W2 = Res("lows"), Res("W2")
            K.dma("pool", W2da[0:64, :], dw2[l], writes=[R_W2])
            K.dma("pool", W2da[64:128, :], aw2[l], writes=[R_W2])
            K.dma("pool", W2g[:, :], gw2[l][0:128, :], writes=[R_W2])
            K.dma("pool", W2gv[0:32, :], gw2[l][128:160, :], writes=[R_W2])
            if l > 0:
                K.dma("pool", W2gv[32:64, :], vw2[l - 1], writes=[R_W2])
            with ExitStack() as p0:
                Wl = sb("Wl", [128, 16, 320], stack=p0)
                Wa = sb("Wa", [128, 16, 320], BF16, stack=p0)
                Wb = sb("Wb", [128, 16, 320], BF16, stack=p0)
                R_Wl, R_Wab = Res("Wl"), Res("Wab")
                if l == 0:
                    K.op("dve", lambda e: e.memset(Wl[:, :, 288:320], 0.0), writes=[R_Wl])
                for src, c0_, w_ in ((dw1[l], 0, 64), (aw1[l], 64, 64), (gw1[l], 128, 160)) + (((vw1[l - 1], 288, 32),) if l > 0 else ()):
                    K.dma("sp", Wl[:, :, c0_:c0_ + w_], src.rearrange("(kc p) n -> p kc n", p=128), writes=[R_Wl])
                for mun, c0_, w_ in (("mu_w", 0, 64), ("mu_a", 64, 64), ("mu_g", 128, 160), ("vmu", 288, 32)):
                    TT(Wb[:, :, c0_:c0_ + w_], Wl[:, :, c0_:c0_ + w_], V(l, mun).unsqueeze(2).to_broadcast([128, 16, w_]),
                       ALU.mult, [R_Wl, R_c], [R_Wab])
                TT(Wa[:], Wl[:], Wb[:], ALU.subtract, [R_Wl, R_Wab], [R_Wab])
                for gi, (c0, n) in enumerate(LGROUPS):
                    for mi, (m0, mw, dst) in enumerate(((0, 128, lows_da), (128, 128, lows_g), (256, 64, lows_gv))):
                        bank = (gi * 3 + mi) % 2
                        for kc in range(16):
                            MM(psb[bank][0:mw, 0:n], Wa[:, kc, m0:m0 + mw], hT[:, kc, c0:c0 + n], kc == 0, False,
                               [R_Wab, R_hT], [R_ps[bank]], sig=False)
                        for kc in range(16):
                            MM(psb[bank][0:mw, 0:n], Wb[:, kc, m0:m0 + mw], hT[:, kc, c0 - 1:c0 - 1 + n], False, kc == 15,
                               [R_Wab, R_hT], [R_ps[bank]])
                        if mi == 0:
                            ACT(dst[0:64, c0:c0 + n], psb[bank][0:64, 0:n], AF.Tanh, [R_ps[bank]], [R_lows])
                            ACT(dst[64:128, c0:c0 + n], psb[bank][64:128, 0:n], AF.Identity, [R_ps[bank]], [R_lows])
                        elif mi == 1:
                            ACT(dst[:, c0:c0 + n], psb[bank][:, 0:n], AF.Sigmoid, [R_ps[bank]], [R_lows])
                        else:
                            ACT(dst[0:32, c0:c0 + n], psb[bank][0:32, 0:n], AF.Sigmoid, [R_ps[bank]], [R_lows])
                            ACT(dst[32:64, c0:c0 + n], psb[bank][32:64, 0:n], AF.Identity, [R_ps[bank]], [R_lows])
                K.barrier()
            names = "rr kk2 vv lw kn bb gg t1 t2 t3".split()
            B = {n: sb("rw_" + n, [128, NCOL], stack=ph) for n in names}
            RB = {n: Res("rw_" + n) for n in names}
            wsl = [sb("rwsl%d" % i, [128, 16, 128], BF16, stack=ph) for i in range(2)]
            R_wsl = [Res("rwsl0"), Res("rwsl1")]
            mixc = sb("mixc2", [128, 2049], BF16, stack=ph)
            R_mixc = Res("mixc2")
            S0T = sb("S0T", [128, 64], stack=ph)
            SsT = sb("SsT", [128, 64], stack=ph)
            Sin = sb("Sin", [64, 2, 64], stack=ph)
            Sout = sb("Sout", [64, 2, 64], stack=ph)
            gbt = sb("gbt", [128, 128], stack=ph)
            bbt = sb("bbt", [128, 128], stack=ph)
            R_S0T, R_SsT, R_Sin, R_Sout, R_gb = Res("S0T"), Res("SsT"), Res("Sin"), Res("Sout"), Res("gbt")
            cn = "lp P Pi Pp Rt At Bt Kt Bh Kh rkp".split()
            CB = {n: sb("ck_" + n, [128, 128], stack=ph) for n in cn}
            RC = {n: Res("ck_" + n) for n in cn}
            tmn = "Bhm Khm Vtm yf yn".split()
            TM = {n: sb("tm_" + n, [128, 128], stack=ph) for n in tmn}
            RT = {n: Res("tm_" + n) for n in tmn}
            hn = "Aab Arb Aak Ark X XT X2 XT2 W".split()
            hn = hn + ["Wb"]
            HB = [{n: sb("h%d_%s" % (hh, n), [128, 64 if n in ("W", "Wb") else 128],
                         F32R if n in ("Aab", "X", "XT", "X2", "XT2", "Wb") else F32, stack=ph) for n in hn} for hh in range(2)]
            RH = [{n: Res("h%d_%s" % (hh, n)) for n in hn} for hh in range(2)]
            stat = sb("stat", [128, 2, 6], stack=ph)
            mv = sb("mv", [128, 2, 2], stack=ph)
            rk = sb("rk", [128, 2], stack=ph)
            R_stat, R_mv, R_rk = Res("stat"), Res("mv"), Res("rk")

            def chunk(c, cs, C, St, R_St, zero_state, mcol):
                sl = slice(cs, cs + C)
                lp, P, Pi, Pp = CB["lp"], CB["P"], CB["Pi"], CB["Pp"]
                K.op("dve", lambda e: e.tensor_tensor_scan(out=lp[:, 0:C], data0=ones_f[:, 0:C], data1=B["lw"][:, sl],
                                                           initial=0.0, op0=ALU.mult, op1=ALU.add),
                     reads=[RB["lw"], R_c], writes=[RC["lp"]])
                ACT(P[:, 0:C], lp[:, 0:C], AF.Exp, [RC["lp"]], [RC["P"]])
                ACT(Pi[:, 0:C], lp[:, 0:C], AF.Exp, [RC["lp"]], [RC["Pi"]], scale=-1.0)
                TT(Pp[:, 0:C], lp[:, 0:C], B["lw"][:, sl], ALU.subtract, [RC["lp"], RB["lw"]], [RC["Pp"]])
                ACT(Pp[:, 0:C], Pp[:, 0:C], AF.Exp, [RC["Pp"]], [RC["Pp"]])
                TT(CB["Rt"][:, 0:C], B["rr"][:, sl], P[:, 0:C], ALU.mult, [RB["rr"], RC["P"]], [RC["Rt"]])
                STT(CB["At"][:, 0:C], B["kn"][:, sl], -1.0, Pp[:, 0:C], ALU.mult, ALU.mult, [RB["kn"], RC["Pp"]], [RC["At"]])
                TT(CB["Bt"][:, 0:C], B["bb"][:, sl], Pi[:, 0:C], ALU.mult, [RB["bb"], RC["Pi"]], [RC["Bt"]])
                TT(CB["Kt"][:, 0:C], B["kk2"][:, sl], Pi[:, 0:C], ALU.mult, [RB["kk2"], RC["Pi"]], [RC["Kt"]])
                TS(CB["Bh"][:, 0:C], CB["Bt"][:, 0:C], P[:, C - 1:C], None, ALU.mult, None, [RC["Bt"], RC["P"]], [RC["Bh"]])
                TS(CB["Kh"][:, 0:C], CB["Kt"][:, 0:C], P[:, C - 1:C], None, ALU.mult, None, [RC["Kt"], RC["P"]], [RC["Kh"]])
                STT(CB["rkp"][:, 0:C], B["rr"][:, sl], V(l, "r_k", c), B["kk2"][:, sl], ALU.mult, ALU.mult,
                    [RB["rr"], RB["kk2"], R_c], [RC["rkp"]])
                for srcb, Rsrc, dstn in ((CB["Bh"][:, 0:C], RC["Bh"], "Bhm"), (CB["Kh"][:, 0:C], RC["Kh"], "Khm"),
                                         (B["vv"][:, sl], RB["vv"], "Vtm")):
                    TR(psb[3][0:C, 0:128], srcb, ident[:], [Rsrc, R_c], [R_ps[3]])
                    ACT(TM[dstn][0:C, :], psb[3][0:C, 0:128], AF.Identity, [R_ps[3]], [RT[dstn]])
                MM(psb[3][0:C, 0:2], CB["rkp"][:, 0:C], blockones[:, 0:128:64], True, True, [RC["rkp"], R_c], [R_ps[3]])
                ACT(rk[0:C, :], psb[3][0:C, 0:2], AF.Identity, [R_ps[3]], [R_rk])
                for hh in range(2):
                    pb = 64 * hh
                    bA, bB = 4 + hh, 6 + hh
                    H, RHh = HB[hh], RH[hh]
                    Bt_, Kt_, At_, Rt_ = CB["Bt"][pb:pb + 64, 0:C], CB["Kt"][pb:pb + 64, 0:C], CB["At"][pb:pb + 64, 0:C], CB["Rt"][pb:pb + 64, 0:C]
                    rd = [RC["Bt"], RC["Kt"], RC["At"], RC["Rt"]]
                    MM(psb[bA][0:C, 0:C], Bt_, At_, True, True, rd, [R_ps[bA]], sig=False)
                    MM(psb[bA][0:C, 128:128 + C], Bt_, Rt_, True, True, rd, [R_ps[bA]], sig=False)
                    MM(psb[bA][0:C, 256:256 + C], Kt_, At_, True, True, rd, [R_ps[bA]], sig=False)
                    MM(psb[bA][0:C, 384:384 + C], Kt_, Rt_, True, True, rd, [R_ps[bA]])
                    MM(psb[bB][0:C, 0:C], At_, Bt_, True, True, rd, [R_ps[bB]])
                    TT(H["Aab"][0:C, 0:C], psb[bA][0:C, 0:C], m_lt[0:C, 0:C], ALU.mult, [R_ps[bA], R_c], [RHh["Aab"]])
                    TT(H["Arb"][0:C, 0:C], psb[bA][0:C, 128:128 + C], m_le[0:C, 0:C], ALU.mult, [R_ps[bA], R_c], [RHh["Arb"]])
                    TT(H["Aak"][0:C, 0:C], psb[bA][0:C, 256:256 + C], m_lt[0:C, 0:C], ALU.mult, [R_ps[bA], R_c], [RHh["Aak"]])
                    TT(H["Ark"][0:C, 0:C], psb[bA][0:C, 384:384 + C], m_le[0:C, 0:C], ALU.mult, [R_ps[bA], R_c], [RHh["Ark"]])
                    TT(H["X"][0:C, 0:C], psb[bB][0:C, 0:C], m_gt[0:C, 0:C], ALU.mult, [R_ps[bB], R_c], [RHh["X"]])
                for hh in range(2):
                    pb = 64 * hh
                    bB = 6 + hh
                    H, RHh = HB[hh], RH[hh]
                    At_ = CB["At"][pb:pb + 64, 0:C]
                    if not zero_state:
                        MM(psb[bB][0:C, 128:192], At_, St[pb:pb + 64, :], True, False, [RC["At"], R_St], [R_ps[bB]], sig=False)
                    MM(psb[bB][0:C, 128:192], H["Aak"][0:C, 0:C], TM["Vtm"][0:C, pb:pb + 64], zero_state, True,
                       [RHh["Aak"], RT["Vtm"]], [R_ps[bB]])
                    ACT(H["W"][0:C, :], psb[bB][0:C, 128:192], AF.Identity, [R_ps[bB]], [RHh["W"]])
                    if C > 1:
                        ACT(H["Wb"][0:C, :], psb[bB][0:C, 128:192], AF.Identity, [R_ps[bB]], [RHh["Wb"]])
                nlev = 0
                while (1 << nlev) < C:
                    nlev += 1
                curX = ["X", "Aab"]
                nxtX = ["X2", "XT2"]
                for lev in range(nlev):
                    lastlev = (lev == nlev - 1)
                    for hh in range(2):
                        bB = 6 + hh
                        H, RHh = HB[hh], RH[hh]
                        Xn, XTn = curX
                        MM(psb[bB][0:C, 128:192], H[XTn][0:C, 0:C], H["Wb"][0:C, :], True, True, [RHh[XTn], RHh["Wb"]], [R_ps[bB]])
                        bA = 4 + hh
                        if not lastlev:
                            MM(psb[bA][0:C, 0:C], H[XTn][0:C, 0:C], H[Xn][0:C, 0:C], True, True, [RHh[XTn], RHh[Xn]], [R_ps[bA]])
                            MM(psb[hh][0:C, 0:C], H[Xn][0:C, 0:C], H[XTn][0:C, 0:C], True, True, [RHh[XTn], RHh[Xn]], [R_ps[hh]])
                        if not lastlev:
                            TT(H["Wb"][0:C, :], H["W"][0:C, :], psb[bB][0:C, 128:192], ALU.add, [RHh["W"], R_ps[bB]], [RHh["Wb"]])
                        TT(H["W"][0:C, :], H["W"][0:C, :], psb[bB][0:C, 128:192], ALU.add, [RHh["W"], R_ps[bB]], [RHh["W"]])
                        if not lastlev:
                            ACT(H[nxtX[0]][0:C, 0:C], psb[bA][0:C, 0:C], AF.Identity, [R_ps[bA]], [RHh[nxtX[0]]])
                            ACT(H[nxtX[1]][0:C, 0:C], psb[hh][0:C, 0:C], AF.Identity, [R_ps[hh]], [RHh[nxtX[1]]])
                    if lev == 0:
                        curX, nxtX = ["X2", "XT2"], ["X", "XT"]
                    else:
                        curX, nxtX = nxtX, curX
                for hh in range(2):
                    pb = 64 * hh
                    H, RHh = HB[hh], RH[hh]
                    osl = psb[hh][0:C, 0:64]
                    if not zero_state:
                        MM(osl, CB["Rt"][pb:pb + 64, 0:C], St[pb:pb + 64, :], True, False, [RC["Rt"], R_St], [R_ps[hh]], sig=False)
                    MM(osl, H["Arb"][0:C, 0:C], H["W"][0:C, :], zero_state, False, [RHh["Arb"], RHh["W"]], [R_ps[hh]], sig=False)
                    MM(osl, H["Ark"][0:C, 0:C], TM["Vtm"][0:C, pb:pb + 64], False, True, [RHh["Ark"], RT["Vtm"]], [R_ps[hh]])
                for hh in range(2):
                    pb = 64 * hh
                    H, RHh = HB[hh], RH[hh]
                    MM(psb[3][pb:pb + 64, 0:64], TM["Bhm"][0:C, pb:pb + 64], H["W"][0:C, :], True, False,
                       [RT["Bhm"], RHh["W"]], [R_ps[3]], sig=False)
                    MM(psb[3][pb:pb + 64, 0:64], TM["Khm"][0:C, pb:pb + 64], TM["Vtm"][0:C, pb:pb + 64], False, True,
                       [RT["Khm"], RT["Vtm"]], [R_ps[3]])
                if zero_state:
                    ACT(St[:, :], psb[3][:, 0:64], AF.Identity, [R_ps[3]], [R_St])
                else:
                    STT(St[:, :], St[:, :], CB["P"][:, C - 1:C], psb[3][:, 0:64], ALU.mult, ALU.add, [R_St, RC["P"], R_ps[3]], [R_St])
                for hh in range(2):
                    K.op("dve", lambda e, hh=hh: e.bn_stats(out=stat[0:C, hh, :], in_=psb[hh][0:C, 0:64]),
                         reads=[R_ps[hh]], writes=[R_stat])
                    K.op("dve", lambda e, hh=hh: e.bn_aggr(out=mv[0:C, hh, :], in_=stat[0:C, hh, :]), reads=[R_stat], writes=[R_mv])
                rsqrt(mv[0:C, :, 1], mv[0:C, :, 1], GN_EPS, [R_mv], [R_mv])
                for hh in range(2):
                    TS(TM["yn"][0:C, hh * 64:(hh + 1) * 64], psb[hh][0:C, 0:64], mv[0:C, hh, 0:1], mv[0:C, hh, 1:2],
                       ALU.subtract, ALU.mult, [R_ps[hh], R_mv], [RT["yn"]])
                TT(TM["yn"][0:C, :], TM["yn"][0:C, :], gbt[0:C, :], ALU.mult, [RT["yn"], R_gb], [RT["yn"]])
                TT(TM["yn"][0:C, :], TM["yn"][0:C, :], bbt[0:C, :], ALU.add, [RT["yn"], R_gb], [RT["yn"]])
                for hh in range(2):
                    STT(TM["yf"][0:C, hh * 64:(hh + 1) * 64], TM["Vtm"][0:C, hh * 64:(hh + 1) * 64], rk[0:C, hh:hh + 1],
                        TM["yn"][0:C, hh * 64:(hh + 1) * 64], ALU.mult, ALU.add, [RT["Vtm"], R_rk, RT["yn"]], [RT["yf"]])
                TR(psb[3][:, 64:64 + C], TM["yf"][0:C, :], ident[0:C, 0:C], [RT["yf"], R_c], [R_ps[3]])
                TT(mixc[:, mcol:mcol + C], psb[3][:, 64:64 + C], B["gg"][:, sl], ALU.mult, [R_ps[3], RB["gg"]], [R_mixc])

            for c in range(rw_c):
                mine = c < 4
                lo = 1 if mine else 2049
                lg = LGROUPS if mine else LGROUPS[4:]
                tl = NTILES if mine else NTILES[4:]
                for wi, (cb, dstn, mun) in enumerate(((3072, "rr", "mu_r"), (4096, "kk2", "mu_k"), (5120, "vv", "mu_v"))):
                    proj_chunk(l, wsl[wi % 2], R_wsl[wi % 2], cb + c * 128, B["t1"], RB["t1"], (0, 1), tiles=tl)
                    TT(B["t2"][:, lo - 1:NCOL - 1], B["t1"][:, lo - 1:NCOL - 1], B["t1"][:, lo:NCOL], ALU.subtract, [RB["t1"]], [RB["t2"]])
                    STT(B[dstn][:, lo:NCOL], B["t2"][:, lo - 1:NCOL - 1], V(l, mun, c), B["t1"][:, lo:NCOL], ALU.mult, ALU.add,
                        [RB["t1"], RB["t2"], R_c], [RB[dstn]])
                for gi, (c0, n) in enumerate(lg):
                    cs_ = slice(c0, c0 + n)
                    MM(psb[0][:, 0:n], W2da[0:64, c * 128:(c + 1) * 128], lows_da[0:64, cs_], True, True, [R_W2, R_lows], [R_ps[0]])
                    ACT(B["lw"][:, cs_], psb[0][:, 0:n], AF.Sigmoid, [R_ps[0], R_c], [RB["lw"]], bias=V(l, "w0", c))
                    MM(psb[1][:, 0:n], W2da[64:128, c * 128:(c + 1) * 128], lows_da[64:128, cs_], True, True, [R_W2, R_lows], [R_ps[1]])
                    ACT(B["t1"][:, cs_], psb[1][:, 0:n], AF.Sigmoid, [R_ps[1], R_c], [RB["t1"]], bias=V(l, "a0", c))
                    MM(psb[0][:, 0:n], W2g[:, c * 128:(c + 1) * 128], lows_g[:, cs_], True, False, [R_W2, R_lows], [R_ps[0]], sig=False)
                    MM(psb[0][:, 0:n], W2gv[0:32, c * 128:(c + 1) * 128], lows_gv[0:32, cs_], False, True, [R_W2, R_lows], [R_ps[0]])
                    ACT(B["gg"][:, cs_], psb[0][:, 0:n], AF.Identity, [R_ps[0]], [RB["gg"]])
                    if l > 0:
                        MM(psb[1][:, 0:n], W2gv[32:64, c * 128:(c + 1) * 128], lows_gv[32:64, cs_], True, True, [R_W2, R_lows], [R_ps[1]])
                        ACT(B["t3"][:, cs_], psb[1][:, 0:n], AF.Sigmoid, [R_ps[1], R_c], [RB["t3"]], bias=V(l, "v0", c))
                TS(B["lw"][:, lo:NCOL], B["lw"][:, lo:NCOL], LNW, None, ALU.mult, None, [RB["lw"]], [RB["lw"]])
                if l == 0:
                    K.dma("sp", vfirst_d[c], B["vv"][:], reads=[RB["vv"]], writes=[R_vfirst])
                else:
                    K.dma("sp", B["t2"][:], vfirst_d[c], reads=[R_vfirst], writes=[RB["t2"]])
                    TT(B["t2"][:, lo:NCOL], B["t2"][:, lo:NCOL], B["vv"][:, lo:NCOL], ALU.subtract, [RB["t2"], RB["vv"]], [RB["t2"]])
                    TT(B["t2"][:, lo:NCOL], B["t2"][:, lo:NCOL], B["t3"][:, lo:NCOL], ALU.mult, [RB["t2"], RB["t3"]], [RB["t2"]])
                    TT(B["vv"][:, lo:NCOL], B["vv"][:, lo:NCOL], B["t2"][:, lo:NCOL], ALU.add, [RB["t2"], RB["vv"]], [RB["vv"]])
                TS(B["kn"][:, lo:NCOL], B["kk2"][:, lo:NCOL], V(l, "k_k", c), None, ALU.mult, None, [RB["kk2"], R_c], [RB["kn"]])
                ACT(B["t2"][:, lo:NCOL], B["kn"][:, lo:NCOL], AF.Square, [RB["kn"]], [RB["t2"]])
                for gi, (c0, n) in enumerate(lg):
                    MM(psb[2][:, 0:n], blockones[:], B["t2"][:, c0:c0 + n], True, True, [R_c, RB["t2"]], [R_ps[2]])
                    rsqrt(B["t3"][:, c0:c0 + n], psb[2][:, 0:n], 1e-24, [R_ps[2]], [RB["t3"]])
                TT(B["kn"][:, lo:NCOL], B["kn"][:, lo:NCOL], B["t3"][:, lo:NCOL], ALU.mult, [RB["kn"], RB["t3"]], [RB["kn"]])
                TT(B["bb"][:, lo:NCOL], B["kn"][:, lo:NCOL], B["t1"][:, lo:NCOL], ALU.mult, [RB["kn"], RB["t1"]], [RB["bb"]])
                TS(B["t2"][:, lo:NCOL], B["t1"][:, lo:NCOL], V(l, "k_a", c), V(l, "k_a", c), ALU.mult, ALU.subtract, [RB["t1"], R_c], [RB["t2"]])
                STT(B["kk2"][:, lo:NCOL], B["t2"][:, lo:NCOL], 1.0, B["kk2"][:, lo:NCOL], ALU.add, ALU.mult, [RB["t2"], RB["kk2"]], [RB["kk2"]])
                for src_, dst_ in (("lnx_g", gbt), ("lnx_b", bbt)):
                    K.op("dve", lambda e, src_=src_, c=c: e.tensor_copy(out=bct[:], in_=V(l, src_, c).to_broadcast([128, 128])),
                         reads=[R_c], writes=[R_bct])
                    MM(psb[7][:, 0:128], bct[:], ident[:], True, True, [R_bct, R_c], [R_ps[7]])
                    ACT(dst_[:, :], psb[7][:, 0:128], AF.Identity, [R_ps[7]], [R_gb])
                for i in range(rw_nch if mine else 0):
                    chunk(c, 1 + 128 * i, 128, S0T, R_S0T, i == 0, 128 * i)
                for hh in range(2 if mine else 0):
                    pb = 64 * hh
                    TR(psb[hh][0:64, 0:64], S0T[pb:pb + 64, :], ident[pb:pb + 64, pb:pb + 64], [R_S0T, R_c], [R_ps[hh]])
                    ACT(Sout[:, hh, :], psb[hh][0:64, 0:64], AF.Identity, [R_ps[hh]], [R_Sout])
                if mine:
                    K.dma("sp", nwkv_p[l][2 * c:2 * c + 2].rearrange("h i j -> i h j"), Sout[:], reads=[R_Sout], writes=[R_out])
                K.dma("sp", Sin[:], swkv_d[l][2 * c:2 * c + 2].rearrange("h i j -> i h j"), writes=[R_Sin])
                for hh in range(2):
                    pb = 64 * hh
                    MM(psb[3][pb:pb + 64, 256:320], Sin[:, hh, :], ident[0:64, 0:64], True, True, [R_Sin, R_c], [R_ps[3]])
                ACT(SsT[:, :], psb[3][:, 256:320], AF.Identity, [R_ps[3]], [R_SsT])
                if rw_sample:
                    chunk(c, SCOL, 1, SsT, R_SsT, False, 2048)
                for hh in range(2):
                    pb = 64 * hh
                    TR(psb[hh][0:64, 0:64], SsT[pb:pb + 64, :], ident[pb:pb + 64, pb:pb + 64], [R_SsT, R_c], [R_ps[hh]])
                    ACT(Sout[:, hh, :], psb[hh][0:64, 0:64], AF.Identity, [R_ps[hh]], [R_Sout])
                K.dma("sp", nwkv_s[l][2 * c:2 * c + 2].rearrange("h i j -> i h j"), Sout[:], reads=[R_Sout], writes=[R_out])
                if mine:
                    for hf in range(2):
                        K.dma("sp", mixh[hf][(4 + c) * 128:(5 + c) * 128, :], mixc[:, hf * TH:(hf + 1) * TH],
                              reads=[R_mixc, R_mixg], writes=[R_mixh])
                K.op("dve", lambda e, c=c: e.tensor_copy(out=mixS[:, 8 + c:9 + c], in_=mixc[:, 2048:2049]), reads=[R_mixc], writes=[R_mixS])
            K.barrier()

    def att_phase(l):
        with ExitStack() as ph:
            qrow = sb("qrow", [1, 1024], stack=ph)
            krow = sb("krow", [1, 1024], stack=ph)
            vrow = sb("vrow", [1, 1024], stack=ph)
            ph1 = ExitStack()
            wsl = [sb("wsl%d" % i, [128, 16, 128], BF16, stack=ph1) for i in range(2)]
            R_wsl = [Res("wsl0"), Res("wsl1")]
            qf = sb("qf", [128, NCOL], stack=ph1)
            kf = sb("kf", [128, NCOL], stack=ph1)
            vf = sb("vf", [128, NCOL], stack=ph1)
            sqf = sb("sqf", [128, NCOL], stack=ph1)
            rs = sb("rs", [128, NCOL], stack=ph1)
            qn = sb("qn", [128, NCOL], BF16, stack=ph1)
            kn = sb("kn", [128, NCOL], BF16, stack=ph1)
            tm = sb("tm", [128, 16, 128], stack=ph1)
            Vp = sb("Vp", [128, 16, 2, 65], BF16, stack=ph1)
            strip = [sb("strip%d" % i, [128, 2048], stack=ph1) for i in range(2)]
            pexp = [sb("pexp%d" % i, [128, 512], stack=ph1) for i in range(2)]
            PT = [sb("PT%d" % i, [128, 512], BF16, stack=ph1) for i in range(2)]
            att_tm = sb("att_tm", [128, 128], stack=ph1)
            rden = sb("rden", [128, 2], stack=ph1)
            mixc = sb("mixc", [128, 2048], BF16, stack=ph1)
            qcol = sb("qcol", [128, 1], stack=ph1)
            R_qf, R_kf, R_vf, R_sqf, R_rs, R_qn, R_kn, R_tm, R_Vp = [Res(n) for n in
                                                                      "qf kf vf sqf rs qn kn tm Vp".split()]
            R_strip = [Res("strip0"), Res("strip1")]
            R_pexp = [Res("pexp0"), Res("pexp1")]
            R_PT = [Res("PT0"), Res("PT1")]
            R_att, R_rden, R_mixc, R_rows, R_qcol = Res("att_tm"), Res("rden"), Res("mixc"), Res("rows"), Res("qcol")
            K.op("dve", lambda e: e.memset(Vp[:], 1.0), writes=[R_Vp])
            sidx = 0
            for c in range(8):
                mine = c < 4
                tl = NTILES if mine else NTILES[4:]
                ca = slice(0, NCOL) if mine else slice(2048, NCOL)
                proj_chunk(l, wsl[0], R_wsl[0], c * 128, qf, R_qf, (0, 1), tiles=tl)
                proj_chunk(l, wsl[1], R_wsl[1], 1024 + c * 128, kf, R_kf, (0, 1), tiles=tl)
                proj_chunk(l, wsl[0], R_wsl[0], 2048 + c * 128, vf, R_vf, (0, 1), tiles=tl)
                for Xf, R_X, isq in ((qf, R_qf, True), (kf, R_kf, False)):
                    K.op("act", lambda e, Xf=Xf, ca=ca: e.activation(out=sqf[:, ca], in_=Xf[:, ca], func=AF.Square),
                         reads=[R_X], writes=[R_sqf])
                    for nt, (c0, n) in enumerate(tl):
                        K.op("pe", lambda e, c0=c0, n=n: e.matmul(psb[2][:, 0:n], lhsT=blockones[:], rhs=sqf[:, c0:c0 + n],
                                                                  start=True, stop=True),
                             reads=[R_c, R_sqf], writes=[R_ps[2]])
                        rsqrt(rs[:, c0:c0 + n], psb[2][:, 0:n], 64 * EPS, [R_ps[2]], [R_rs])
                    if isq:
                        K.op("dve", lambda e, ca=ca: e.scalar_tensor_tensor(out=qn[:, ca], in0=qf[:, ca], scalar=V(l, "gq"), in1=rs[:, ca],
                                                                     op0=ALU.mult, op1=ALU.mult),
                             reads=[R_qf, R_rs, R_c], writes=[R_qn])
                        K.op("dve", lambda e: e.scalar_tensor_tensor(
                            out=qcol[:], in0=qf[:, SCOL:SCOL + 1], scalar=V(l, "gq"), in1=rs[:, SCOL:SCOL + 1],
                            op0=ALU.mult, op1=ALU.mult), reads=[R_qf, R_rs, R_c], writes=[R_qcol])
                    else:
                        K.op("dve", lambda e, ca=ca: e.scalar_tensor_tensor(out=kf[:, ca], in0=kf[:, ca], scalar=gk8[:, l:l + 1],
                                                                     in1=rs[:, ca], op0=ALU.mult, op1=ALU.mult),
                             reads=[R_kf, R_rs, R_c], writes=[R_kf])
                        K.op("dve", lambda e, ca=ca: e.tensor_copy(out=kn[:, ca], in_=kf[:, ca]), reads=[R_kf], writes=[R_kn])
                for Xf, R_X, dst, srow in ((kf, R_kf, nk_p, krow), (vf, R_vf, nv_p, vrow)):
                    for g in range(4 if mine else 0):
                        for jj in range(4):
                            i_ = 4 * g + jj
                            K.op("pe", lambda e, Xf=Xf, i_=i_, jj=jj: e.transpose(
                                out=psb[3][:, jj * 128:(jj + 1) * 128], in_=Xf[:, 1 + 128 * i_:1 + 128 * (i_ + 1)],
                                identity=ident[:]), reads=[R_X, R_c], writes=[R_ps[3]], sig=(jj == 3))
                        K.op("act", lambda e, g=g: e.copy(
                            out=tm[:, 4 * g:4 * g + 4, :].rearrange("p a b -> p (a b)"), in_=psb[3][:, :]),
                            reads=[R_ps[3]], writes=[R_tm])
                    if mine:
                        K.dma("sp", dst[l].rearrange("(i p) f -> p i f", p=128)[:, :, c * 128:(c + 1) * 128], tm[:],
                              reads=[R_tm], writes=[R_out])
                    if Xf is vf and mine:
                        K.op("dve", lambda e: e.tensor_copy(
                            out=Vp[:, :, :, 0:64], in_=tm[:].rearrange("p i (h e) -> p i h e", h=2)),
                            reads=[R_tm], writes=[R_Vp])
                    K.op("pe", lambda e, Xf=Xf: e.transpose(out=psb[3][0:1, 0:128], in_=Xf[:, SCOL:SCOL + 1],
                                                           identity=ident[:]),
                         reads=[R_X, R_c], writes=[R_ps[3]])
                    K.op("act", lambda e, srow=srow, c=c: e.copy(out=srow[0:1, c * 128:(c + 1) * 128], in_=psb[3][0:1, 0:128]),
                         reads=[R_ps[3]], writes=[R_rows])
                K.op("pe", lambda e: e.transpose(out=psb[3][0:1, 0:128], in_=qcol[:, 0:1], identity=ident[:]),
                     reads=[R_qcol, R_c], writes=[R_ps[3]])
                K.op("act", lambda e, c=c: e.copy(out=qrow[0:1, c * 128:(c + 1) * 128], in_=psb[3][0:1, 0:128]),
                     reads=[R_ps[3]], writes=[R_rows])
                for hh in range(2 if mine else 0):
                    K.dma("sp", strip[hh][:], estrip[2 * c + hh], reads=[R_estrip], writes=[R_strip[hh]])
                for qi in range(16 if mine else 0):
                    for hh in range(2):
                        pb = 64 * hh
                        nkb = qi + 1
                        ob = 6 + hh
                        first = True
                        for g0 in range(0, nkb, 4):
                            kbs = [qi - g0 - s_ for s_ in range(4) if qi - g0 - s_ >= 0]
                            n = 128 * len(kbs)
                            sl = sidx % 2
                            sidx += 1
                            bank = 4 + sl
                            for s_, kb in enumerate(kbs):
                                K.op("pe", lambda e, bank=bank, s_=s_, kb=kb, pb=pb, qi=qi: e.matmul(
                                    psb[bank][:, s_ * 128:(s_ + 1) * 128],
                                    lhsT=kn[pb:pb + 64, 1 + 128 * kb:1 + 128 * (kb + 1)],
                                    rhs=qn[pb:pb + 64, 1 + 128 * qi:1 + 128 * (qi + 1)], start=True, stop=True),
                                    reads=[R_kn, R_qn], writes=[R_ps[bank]], sig=(s_ == len(kbs) - 1))
                            K.op("act", lambda e, bank=bank, sl=sl, n=n: e.activation(
                                out=pexp[sl][:, 0:n], in_=psb[bank][:, 0:n], func=AF.Exp),
                                reads=[R_ps[bank]], writes=[R_pexp[sl]])
                            J0 = 128 * g0
                            K.op("dve", lambda e, sl=sl, n=n, J0=J0, hh=hh: e.tensor_tensor(
                                out=PT[sl][:, 0:n], in0=pexp[sl][:, 0:n], in1=strip[hh][:, J0:J0 + n], op=ALU.mult),
                                reads=[R_pexp[sl], R_strip[hh]], writes=[R_PT[sl]])
                            for s_, kb in enumerate(kbs):
                                last = (g0 + 4 >= nkb) and (s_ == len(kbs) - 1)
                                K.op("pe", lambda e, ob=ob, sl=sl, s_=s_, kb=kb, hh=hh, first=first, last=last: e.matmul(
                                    psb[ob][:, 0:65], lhsT=PT[sl][:, s_ * 128:(s_ + 1) * 128], rhs=Vp[:, kb, hh, :],
                                    start=first, stop=last),
                                    reads=[R_PT[sl], R_Vp], writes=[R_ps[ob]], sig=(s_ == len(kbs) - 1))
                                first = False
                        K.op("dve", lambda e, ob=ob, hh=hh: e.reciprocal(out=rden[:, hh:hh + 1], in_=psb[ob][:, 64:65]),
                             reads=[R_ps[ob]], writes=[R_rden])
                        K.op("dve", lambda e, ob=ob, hh=hh: e.tensor_scalar(
                            out=att_tm[:, hh * 64:(hh + 1) * 64], in0=psb[ob][:, 0:64], scalar1=rden[:, hh:hh + 1],
                            scalar2=None, op0=ALU.mult), reads=[R_ps[ob], R_rden], writes=[R_att])
                    K.op("pe", lambda e: e.transpose(out=psb[3][:, 0:128], in_=att_tm[:], identity=ident[:]),
                         reads=[R_att, R_c], writes=[R_ps[3]])
                    K.op("act", lambda e, qi=qi: e.copy(out=mixc[:, 128 * qi:128 * (qi + 1)], in_=psb[3][:, 0:128]),
                         reads=[R_ps[3]], writes=[R_mixc])
                if mine:
                    for hf in range(2):
                        K.dma("sp", mixh[hf][c * 128:(c + 1) * 128, :], mixc[:, hf * TH:(hf + 1) * TH],
                              reads=[R_mixc, R_mixg], writes=[R_mixh])
            K.dma("sp", nk_s[l], krow[:], reads=[R_rows], writes=[R_out])
            K.dma("sp", nv_s[l], vrow[:], reads=[R_rows], writes=[R_out])
            K.barrier()
            ph1.close()
            qbc = sb("qbc", [128, 1024], stack=ph)
            bdmask = sb("bdmask", [16, 1024], stack=ph)
            K.dma("sp", bdmask[:], cst["bdmask"], writes=[R_c])
            prodt = sb("prodt", [128, 1024], stack=ph)
            kblk = [sb("kblk%d" % i, [128, 1024], stack=ph) for i in range(2)]
            vblk = [sb("vblk%d" % i, [128, 1024], stack=ph) for i in range(2)]
            R_kb = [Res("kblk0"), Res("kblk1")]
            R_vb = [Res("vblk0"), Res("vblk1")]
            sc = sb("sc", [128, 17, 16], stack=ph)
            pS = sb("pS", [128, 17, 16], stack=ph)
            numm = sb("numm", [16, 1024], stack=ph)
            dens = sb("dens", [16, 2], stack=ph)
            arow = sb("arow", [1, 1024], stack=ph)
            R_qbc, R_prod, R_sc, R_pS, R_numm, R_dens, R_arow = [Res(n) for n in
                                                                 "qbc prod sc pS numm dens arow".split()]
            for hf in range(2):
                K.op("pe", lambda e, hf=hf: e.matmul(psb[hf][:, :], lhsT=ones_f[0:1, 0:128],
                                                     rhs=qrow[0:1, hf * 512:(hf + 1) * 512], start=True, stop=True),
                     reads=[R_rows, R_c], writes=[R_ps[hf]])
                K.op("act", lambda e, hf=hf: e.copy(out=qbc[:, hf * 512:(hf + 1) * 512], in_=psb[hf][:, :]),
                     reads=[R_ps[hf]], writes=[R_qbc])
            for blk in range(17):
                m = 128 if blk < 16 else 1
                sl = blk % 2
                if blk < 16:
                    K.dma("sp", kblk[sl][:], ck_d[l][blk * 128:(blk + 1) * 128, :], writes=[R_kb[sl]])
                    K.dma("sp", vblk[sl][:], cv_d[l][blk * 128:(blk + 1) * 128, :], writes=[R_vb[sl]])
                    ksrc, vsrc, Rk, Rv = kblk[sl], vblk[sl], R_kb[sl], R_vb[sl]
                else:
                    ksrc, vsrc, Rk, Rv = krow, vrow, R_rows, R_rows
                K.op("dve", lambda e, ksrc=ksrc, m=m: e.tensor_tensor(out=prodt[0:m, :], in0=ksrc[0:m, :], in1=qbc[0:m, :],
                                                                      op=ALU.mult),
                     reads=[Rk, R_qbc], writes=[R_prod])
                K.op("dve", lambda e, blk=blk, m=m: e.tensor_reduce(
                    out=sc[0:m, blk, :], in_=prodt[0:m, :].rearrange("p (h e) -> p h e", e=64), axis=AX.X, op=ALU.add),
                    reads=[R_prod], writes=[R_sc])
                K.op("act", lambda e, blk=blk, m=m: e.activation(out=pS[0:m, blk, :], in_=sc[0:m, blk, :], func=AF.Exp),
                     reads=[R_sc], writes=[R_pS])
                K.op("dve", lambda e, blk=blk, m=m: e.tensor_tensor(out=pS[0:m, blk, :], in0=pS[0:m, blk, :],
                                                                    in1=EsS[0:m, blk, :], op=ALU.mult),
                     reads=[R_pS, R_c], writes=[R_pS])
                for hf in range(2):
                    K.op("pe", lambda e, blk=blk, m=m, hf=hf, vsrc=vsrc: e.matmul(
                        psb[hf][0:16, :], lhsT=pS[0:m, blk, :], rhs=vsrc[0:m, hf * 512:(hf + 1) * 512],
                        start=(blk == 0), stop=(blk == 16)), reads=[R_pS, Rv], writes=[R_ps[hf]])
                K.op("pe", lambda e, blk=blk, m=m: e.matmul(
                    psb[2][0:16, 0:1], lhsT=pS[0:m, blk, :], rhs=ones_f[0:m, 0:1],
                    start=(blk == 0), stop=(blk == 16)), reads=[R_pS, R_c], writes=[R_ps[2]])
            K.op("dve", lambda e: e.reciprocal(out=dens[:, 0:1], in_=psb[2][0:16, 0:1]), reads=[R_ps[2]], writes=[R_dens])
            for hf in range(2):
                K.op("dve", lambda e, hf=hf: e.scalar_tensor_tensor(
                    out=numm[:, hf * 512:(hf + 1) * 512], in0=psb[hf][0:16, :], scalar=dens[:, 0:1],
                    in1=bdmask[:, hf * 512:(hf + 1) * 512], op0=ALU.mult, op1=ALU.mult),
                    reads=[R_ps[hf], R_dens, R_c], writes=[R_numm])
            for hf in range(2):
                K.op("pe", lambda e, hf=hf: e.matmul(psb[3][0:1, :], lhsT=ones_f[0:16, 0:1],
                                                     rhs=numm[:, hf * 512:(hf + 1) * 512], start=True, stop=True),
                     reads=[R_numm, R_c], writes=[R_ps[3]])
                K.op("act", lambda e, hf=hf: e.copy(out=arow[0:1, hf * 512:(hf + 1) * 512], in_=psb[3][0:1, :]),
                     reads=[R_ps[3]], writes=[R_arow])
            for c in range(8):
                K.op("pe", lambda e, c=c: e.matmul(psb[4][:, c:c + 1], lhsT=arow[0:1, c * 128:(c + 1) * 128],
                                                   rhs=ones_f[0:1, 0:1], start=True, stop=True),
                     reads=[R_arow, R_c], writes=[R_ps[4]])
            K.op("act", lambda e: e.copy(out=mixS[:, 0:8], in_=psb[4][:, 0:8]), reads=[R_ps[4]], writes=[R_mixS])
            K.barrier()

    def bcast_rows(dst, j, which, R_dst, np_):
        for g in range(4):
            for jj in range(4):
                kc = 4 * g + jj
                K.op("dve", lambda e, kc=kc: e.tensor_copy(out=bct[:], in_=MOD(j, which, kc).to_broadcast([128, 128])),
                     reads=[R_mod], writes=[R_bct])
                K.op("pe", lambda e, jj=jj: e.matmul(psb[7][:, jj * 128:(jj + 1) * 128], lhsT=bct[:], rhs=ident[:],
                                                     start=True, stop=True),
                     reads=[R_bct, R_c], writes=[R_ps[7]])
            K.op("act", lambda e, g=g: e.copy(out=dst[0:np_, g * 512:(g + 1) * 512], in_=psb[7][0:np_, :]),
                 reads=[R_ps[7]], writes=[R_dst])

    bct = sb("bct", [128, 128])
    R_bct = Res("bct")

    def gmap(g):
        rho, j = g // 8, g % 8
        return 4 * rho + j if j < 4 else 8 + 4 * rho + (j - 4)

    def wout_phase(l):
        for hf in range(2):
            K.op("pool", lambda e, hf=hf: e.collective_compute("AllGather", ALU.bypass, replica_groups=RG,
                                                               ins=[mixh[hf]], outs=[mixg[hf]]),
                 reads=[R_mixh], writes=[R_mixg, R_cc])
        with ExitStack() as ph:
            mixT = sb("mixT", [128, 16, TH], BF16, stack=ph)
            mA = sb("mA", [128, 16, TH], BF16, stack=ph)
            sel = sb("sel", [128, 2], stack=ph)
            gbc_p = sb("gbc_p", [128, D], stack=ph)
            gbc_s = sb("gbc_s", [1, D], stack=ph)
            wo = [sb("wo%d" % i, [128, 16, 512], BF16, stack=ph) for i in range(2)]
            wos = sb("wos", [128, 16, 512], BF16, stack=ph)
            xq = [sb("xq%d" % i, [128, 512], stack=ph) for i in range(2)]
            R_wo = [Res("wo0"), Res("wo1")]
            R_wos = Res("wos")
            R_xq = [Res("xq0"), Res("xq1")]
            R_mT, R_mA, R_g, R_sel = Res("mixTs"), Res("mA"), Res("gbc"), Res("sel")
            K.dma("sp", sel[:], sel_d, writes=[R_sel])
            K.dma("sp", mixT[:], mixg[0].rearrange("(c p) t -> p c t", p=128), reads=[R_mixg], writes=[R_mT])
            K.dma("sp", mA[:], mixg[1].rearrange("(c p) t -> p c t", p=128), reads=[R_mixg], writes=[R_mA])
            TS(mixT[:], mixT[:], sel[:, 0:1], None, ALU.mult, None, [R_mT, R_sel], [R_mT])
            STT(mixT[:], mA[:], sel[:, 1:2], mixT[:], ALU.mult, ALU.add, [R_mA, R_sel, R_mT], [R_mT])
            bcast_rows(gbc_p, 0, "gt1", R_g, 128)
            bcast_rows(gbc_s, 1, "gt1", R_g, 1)
            if l == 0:
                xs_p = lambda i: xph[i * 128:(i + 1) * 128, :]
            else:
                xs_p = lambda i: xhp[i // 2][(i % 2) * 128:(i % 2) * 128 + 128, :]
            xs_s = xs if l == 0 else xsb
            it = 0
            for ng in range(4):
                sl = ng % 2
                K.dma("pool", wo[sl][:], w_view(w_out[l], ng * 512, 512), writes=[R_wo[sl]])
                K.dma("pool", wos[:], w_view(w_out_perm[l], ng * 512, 512), writes=[R_wos])
                for i in list(range(8)) + [16]:
                    np_ = 128 if i < 16 else 1
                    col0 = 128 * i
                    xsrc = xs_p(i)[:, ng * 512:(ng + 1) * 512] if i < 16 else xs_s[:, ng * 512:(ng + 1) * 512]
                    xdst = x1buf[i * 128:(i + 1) * 128, ng * 512:(ng + 1) * 512] if i < 16 else x1buf[TH:TH + 1, ng * 512:(ng + 1) * 512]
                    gb = gbc_p if i < 16 else gbc_s
                    q_ = it % 2
                    it += 1
                    bank = q_
                    K.dma("sp", xq[q_][0:np_, :], xsrc, reads=[R_xh], writes=[R_xq[q_]])
                    for kc in range(16):
                        if i < 16:
                            MM(psb[bank][0:np_, :], mixT[:, kc, col0:col0 + np_], wo[sl][:, gmap(kc), :], kc == 0, kc == 15,
                               [R_mT, R_wo[sl]], [R_ps[bank]])
                        else:
                            MM(psb[bank][0:1, :], mixS[:, kc:kc + 1], wos[:, kc, :], kc == 0, kc == 15,
                               [R_mixS, R_wos], [R_ps[bank]])
                    K.op("dve", lambda e, bank=bank, np_=np_, gb=gb, ng=ng, q_=q_: e.tensor_tensor(
                        out=psb[bank][0:np_, :], in0=psb[bank][0:np_, :], in1=gb[0:np_, ng * 512:(ng + 1) * 512], op=ALU.mult),
                        reads=[R_ps[bank], R_g], writes=[R_ps[bank]])
                    K.op("dve", lambda e, bank=bank, np_=np_, q_=q_: e.tensor_tensor(
                        out=xq[q_][0:np_, :], in0=psb[bank][0:np_, :], in1=xq[q_][0:np_, :], op=ALU.add),
                        reads=[R_ps[bank], R_xq[q_]], writes=[R_xq[q_]])
                    K.dma("sp", xdst, xq[q_][0:np_, :], reads=[R_xq[q_]], writes=[R_x1buf])
            K.barrier()

    FGROUPS = [(1, 512), (513, 512)]

    def ffn_phase(l, last):
        with ExitStack() as ph:
            actT = sb("actT", [128, 44, 513], BF16, stack=ph)
            sgs = sb("sgs", [128, 2], stack=ph)
            R_sgs = Res("sgs")
            gbc_p = sb("gbc2_p", [128, D], stack=ph)
            gbc_s = sb("gbc2_s", [1, D], stack=ph)
            wg = [sb("wg%d" % i, [128, 16, 128], BF16, stack=ph) for i in range(2)]
            wu = [sb("wu%d" % i, [128, 16, 128], BF16, stack=ph) for i in range(2)]
            wd = [sb("wd%d" % i, [128, 11, 512], BF16, stack=ph) for i in range(2)]
            sg = [sb("sg%d" % i, [128, 512], stack=ph) for i in range(2)]
            xq = [sb("xq2_%d" % i, [128, 512], stack=ph) for i in range(5)]
            R_wg = [Res("wg0"), Res("wg1")]
            R_wu = [Res("wu0"), Res("wu1")]
            R_wd = [Res("wd0"), Res("wd1")]
            R_sg = [Res("sg0"), Res("sg1")]
            R_xq = [Res("xq2_%d" % i) for i in range(5)]
            R_aT, R_g = Res("actT"), Res("gbc2")
            bcast_rows(gbc_p, 0, "gt2", R_g, 128)
            bcast_rows(gbc_s, 1, "gt2", R_g, 1)
            wdi = 0
            for gi, (c0, n) in enumerate(FGROUPS):
                for fc in range(44):
                    sl = fc % 2
                    K.dma("pool", wg[sl][:], w_view(w_gu[l], fc * 128, 128), writes=[R_wg[sl]])
                    K.dma("pool", wu[sl][:], w_view(w_gu[l], DFF + fc * 128, 128), writes=[R_wu[sl]])
                    bg, bu = 4 + sl, 6 + sl
                    for kc in range(16):
                        K.op("pe", lambda e, bg=bg, kc=kc, sl=sl, c0=c0, n=n: e.matmul(
                            psb[bg][:, 0:n], lhsT=wg[sl][:, kc, :], rhs=hT[:, kc, c0:c0 + n], start=(kc == 0), stop=(kc == 15)),
                            reads=[R_wg[sl], R_hT], writes=[R_ps[bg]], sig=(kc == 15))
                    for kc in range(16):
                        K.op("pe", lambda e, bu=bu, kc=kc, sl=sl, c0=c0, n=n: e.matmul(
                            psb[bu][:, 0:n], lhsT=wu[sl][:, kc, :], rhs=hT[:, kc, c0:c0 + n], start=(kc == 0), stop=(kc == 15)),
                            reads=[R_wu[sl], R_hT], writes=[R_ps[bu]], sig=(kc == 15))
                    K.op("act", lambda e, bg=bg, sl=sl, n=n: e.activation(out=sg[sl][:, 0:n], in_=psb[bg][:, 0:n], func=AF.Silu),
                         reads=[R_ps[bg]], writes=[R_sg[sl]])
                    K.op("dve", lambda e, bu=bu, sl=sl, n=n, fc=fc: e.tensor_tensor(
                        out=actT[:, fc, 0:n], in0=sg[sl][:, 0:n], in1=psb[bu][:, 0:n], op=ALU.mult),
                        reads=[R_sg[sl], R_ps[bu]], writes=[R_aT])
                    if gi == 1:
                        gb_, ub_ = 2 * sl, 2 * sl + 1
                        for kc in range(16):
                            MM(psb[gb_][:, 0:1], wg[sl][:, kc, :], hT[:, kc, SCOL:SCOL + 1], kc == 0, kc == 15,
                               [R_wg[sl], R_hT], [R_ps[gb_]])
                        for kc in range(16):
                            MM(psb[ub_][:, 0:1], wu[sl][:, kc, :], hT[:, kc, SCOL:SCOL + 1], kc == 0, kc == 15,
                               [R_wu[sl], R_hT], [R_ps[ub_]])
                        ACT(sgs[:, sl:sl + 1], psb[gb_][:, 0:1], AF.Silu, [R_ps[gb_]], [R_sgs])
                        TT(actT[:, fc, 512:513], sgs[:, sl:sl + 1], psb[ub_][:, 0:1], ALU.mult, [R_sgs, R_ps[ub_]], [R_aT])
                ntile = 4 if gi == 0 else 5
                for ng in range(4):
                    for fs in range(4):
                        sl = wdi % 2
                        wdi += 1
                        K.dma("pool", wd[sl][:], w_down[l].rearrange("(fc p) n -> p fc n", p=128)[:, fs * 11:(fs + 1) * 11, ng * 512:(ng + 1) * 512],
                              writes=[R_wd[sl]])
                        for ti in range(ntile):
                            np_ = 128 if ti < 4 else 1
                            for f_ in range(11):
                                fc = fs * 11 + f_
                                K.op("pe", lambda e, ti=ti, fc=fc, f_=f_, sl=sl, np_=np_: e.matmul(
                                    psb[ti][0:np_, :], lhsT=actT[:, fc, ti * 128:ti * 128 + np_], rhs=wd[sl][:, f_, :],
                                    start=(fc == 0), stop=(fc == 43)), reads=[R_aT, R_wd[sl]], writes=[R_ps[ti]],
                                    sig=(f_ == 10))
                    for ti in range(ntile):
                        np_ = 128 if ti < 4 else 1
                        gb = gbc_p if ti < 4 else gbc_s
                        if ti < 4:
                            r0 = gi * 512 + ti * 128
                            xsrc = x1buf[r0:r0 + 128, ng * 512:(ng + 1) * 512]
                            xdst = (y_p[r0:r0 + 128, :] if last else xhp[r0 // 256][r0 % 256:r0 % 256 + 128, :])[:, ng * 512:(ng + 1) * 512]
                        else:
                            xsrc = x1buf[TH:TH + 1, ng * 512:(ng + 1) * 512]
                            xdst = (y_s if last else xsb)[:, ng * 512:(ng + 1) * 512]
                        K.dma("sp", xq[ti][0:np_, :], xsrc, reads=[R_x1buf], writes=[R_xq[ti]])
                        K.op("dve", lambda e, ti=ti, np_=np_, gb=gb, ng=ng: e.tensor_tensor(
                            out=psb[ti][0:np_, :], in0=psb[ti][0:np_, :], in1=gb[0:np_, ng * 512:(ng + 1) * 512], op=ALU.mult),
                            reads=[R_ps[ti], R_g], writes=[R_ps[ti]])
                        K.op("dve", lambda e, ti=ti, np_=np_: e.tensor_tensor(
                            out=xq[ti][0:np_, :], in0=psb[ti][0:np_, :], in1=xq[ti][0:np_, :], op=ALU.add),
                            reads=[R_ps[ti], R_xq[ti]], writes=[R_xq[ti]])
                        K.dma("sp", xdst, xq[ti][0:np_, :], reads=[R_xq[ti]], writes=[R_out if last else R_xh])
            K.barrier()

    for l in range(nl):
        adaln_phase(l)
        if l == 0:
            xsrc_p = lambda i: xp[i * 128:(i + 1) * 128, :]
        else:
            xsrc_p = lambda i: xg[(i % 8) // 2][(i // 8) * 256 + (i % 2) * 128:(i // 8) * 256 + (i % 2) * 128 + 128, :]
        xsrc_s = xs if l == 0 else xsb
        norm_phase(l, xsrc_p, xsrc_s, A1, "sh1", True)
        if stop_after == "norm1":
            break
        att_phase(l)
        if stop_after == "att":
            break
        if "rwkv" not in skip:
            rwkv_phase(l)
        if stop_after == "rwkv":
            break
        wout_phase(l)
        if stop_after == "wout":
            break
        norm_phase(l, (lambda i: x1buf[i * 128:(i + 1) * 128, :]), x1buf[TH:TH + 1, :], A2, "sh2", False, ntp=8)
        if stop_after == "norm2":
            break
        ffn_phase(l, l == NL - 1)
        if l < NL - 1:
            for k in range(4):
                K.op("pool", lambda e, k=k: e.collective_compute("AllGather", ALU.bypass, replica_groups=RG,
                                                                ins=[xhp[k]], outs=[xg[k]]),
                     reads=[R_xh], writes=[R_xfull, R_cc])
            K.barrier()

    K.barrier()
    block = es.enter_context(nc.Block())

    @block.tensor
    def _(e):
        K.emit(e, "pe")

    @block.scalar
    def _(e):
        K.emit(e, "act")

    @block.vector
    def _(e):
        K.emit(e, "dve")

    @block.gpsimd
    def _(e):
        K.emit(e, "pool")

    @block.sync
    def _(e):
        K.emit(e, "sp")

    es.close()
    return nc


def perms(r):
    cp = [4 * r + j for j in range(4)] + [4 * (1 - r) + j for j in range(4)]
    featp = np.concatenate([np.arange(128) + 128 * c for c in cp])
    headp = np.array([2 * c + h for c in cp for h in range(2)])
    return cp, featp, headp


def rank_shared(inp, r):
    f = lambda a: np.ascontiguousarray(np.asarray(a, dtype=np.float32))
    cp, featp, headp = perms(r)
    m = {}
    cols = np.concatenate([g * 1024 + featp for g in range(6)])
    m["w_in"] = f(inp["w_in"][:, :, cols])
    rows = np.concatenate([featp, 1024 + featp])
    m["w_out_perm"] = f(inp["w_out"][:, rows, :])
    m["decay_w2"] = f(inp["decay_w2"][:, :, featp])
    m["aaa_w2"] = f(inp["aaa_w2"][:, :, featp])
    m["gate_w2"] = f(inp["gate_w2"][:, :, featp])
    m["vres_w2"] = f(inp["vres_w2"][:, :, featp])
    m["relb"] = f(inp["rel_bias"][:, headp])
    vecs = np.zeros((128, NL, NV), np.float32)

    def put(l, name, arr):
        o, w = VOFF[name]
        vecs[:, l, o:o + w] = arr
    for l in range(NL):
        put(l, "ada_b", fm(inp["ada_b"][l], 96))
        put(l, "g1", fm(inp["norm1_g"][l], 16))
        put(l, "g2", fm(inp["norm2_g"][l], 16))
        put(l, "mu_w", fm(inp["mu_wag"][l, 0], 16))
        put(l, "mu_a", fm(inp["mu_wag"][l, 1], 16))
        put(l, "mu_g", fm(inp["mu_wag"][l, 2], 16))
        put(l, "mu_r", fm(inp["mu_rkv"][l, 0], 8)[:, cp])
        put(l, "mu_k", fm(inp["mu_rkv"][l, 1], 8)[:, cp])
        put(l, "mu_v", fm(inp["mu_rkv"][l, 2], 8)[:, cp])
        put(l, "w0", fm(inp["decay_w0"][l], 8)[:, cp])
        put(l, "a0", fm(inp["aaa_a0"][l], 8)[:, cp])
        if l > 0:
            put(l, "vmu", fm(inp["vres_mu"][l - 1], 16))
            put(l, "v0", fm(inp["vres_v0"][l - 1], 8)[:, cp])
        put(l, "k_k", fm(inp["k_k"][l], 8)[:, cp])
        put(l, "k_a", fm(inp["k_a"][l], 8)[:, cp])
        put(l, "r_k", fm(np.asarray(inp["r_k"][l]).reshape(-1), 8)[:, cp])
        put(l, "lnx_g", fm(inp["lnx_g"][l], 8)[:, cp])
        put(l, "lnx_b", fm(inp["lnx_b"][l], 8)[:, cp])
        put(l, "gq", np.tile(np.asarray(inp["q_norm_g"][l], np.float32), 2)[:, None])
        put(l, "gk", np.tile(np.asarray(inp["k_norm_g"][l], np.float32), 2)[:, None])
    m["vecs"] = vecs
    sel = np.zeros((128, 2), np.float32)
    sel[:, r] = 1.0
    m["sel"] = sel
    for n in ("ada_w", "w_out", "w_gu", "w_down", "decay_w1", "aaa_w1", "gate_w1", "vres_w1"):
        m[n] = f(inp[n])
    return m


def make_in_map(inp, core, consts, shared):
    b = core % 4
    r = core // 4
    i = core
    f = lambda a: np.ascontiguousarray(np.asarray(a, dtype=np.float32))
    cp, featp, headp = perms(r)
    m = dict(shared[r])
    m["xp"] = f(inp["x_prompt"][b])
    m["xph"] = f(inp["x_prompt"][b, TH * r:TH * (r + 1)])
    m["xs"] = f(inp["x_sample"][i])
    cT = np.stack([fm(inp["c_prompt"][b], 16), fm(inp["c_sample"][i], 16)], axis=-1)
    m["cT"] = f(cT)
    m["ck"] = f(np.asarray(inp["cache_k"])[:, i][:, :, headp, :].reshape(NL, 2048, 1024))
    m["cv"] = f(np.asarray(inp["cache_v"])[:, i][:, :, headp, :].reshape(NL, 2048, 1024))
    m["swkv"] = f(np.asarray(inp["state_wkv"])[:, i][:, headp])
    m["sshT"] = f(np.stack([fm(inp["state_shift"][l, i], 16) for l in range(NL)]))
    for n, a in consts.items():
        m["c_" + n] = a
    return m


def unfm(a):
    return np.ascontiguousarray(a.T).reshape(-1)


def assemble(R):
    y_p = np.zeros((4, T, D), np.float32)
    nk_p = np.zeros((NL, 4, T, 16, 64), np.float32)
    nv_p = np.zeros((NL, 4, T, 16, 64), np.float32)
    nwkv_p = np.zeros((NL, 4, 16, 64, 64), np.float32)
    nk_s = np.zeros((NL, 8, 1, 16, 64), np.float32)
    nv_s = np.zeros((NL, 8, 1, 16, 64), np.float32)
    nwkv_s = np.zeros((NL, 8, 16, 64, 64), np.float32)
    for core in range(8):
        if R[core] is None:
            continue
        b, r = core % 4, core // 4
        cp, featp, headp = perms(r)
        y_p[b, TH * r:TH * (r + 1)] = R[core]["y_p"]
        nk_p[:, b, :, 8 * r:8 * r + 8, :] = np.asarray(R[core]["nk_p"]).reshape(NL, T, 8, 64)
        nv_p[:, b, :, 8 * r:8 * r + 8, :] = np.asarray(R[core]["nv_p"]).reshape(NL, T, 8, 64)
        nwkv_p[:, b, 8 * r:8 * r + 8] = R[core]["nwkv_p"]
        nk_s[:, core, :, headp, :] = np.moveaxis(np.asarray(R[core]["nk_s"]).reshape(NL, 1, 16, 64), 2, 0)
        nv_s[:, core, :, headp, :] = np.moveaxis(np.asarray(R[core]["nv_s"]).reshape(NL, 1, 16, 64), 2, 0)
        nwkv_s[:, core, headp] = np.asarray(R[core]["nwkv_s"])
    y_s = np.stack([R[i]["y_s"] for i in range(8)]).astype(np.float32)
    nsh_p = np.stack([np.stack([unfm(R[b]["nsh_p"][l]) for b in range(4)]) for l in range(NL)])
    nsh_s = np.stack([np.stack([unfm(R[i]["nsh_s"][l]) for i in range(8)]) for l in range(NL)])
    outs = (y_p, y_s, nk_p, nv_p, nwkv_p, nsh_p, nk_s, nv_s, nwkv_s, nsh_s)
    return tuple(np.ascontiguousarray(o, dtype=np.float32) for o in outs)


def kernel(**inputs):
    consts = host_consts()
    inp = {k: np.asarray(v) for k, v in inputs.items()}
    nc = build()
    shared = [rank_shared(inp, r) for r in range(2)]
    in_maps = [make_in_map(inp, c, consts, shared) for c in range(8)]
    res = run_bass_kernel_spmd(nc, in_maps, core_ids=list(range(8)))
    return assemble(res.results)
```
